# Optimizing a Trainium2 kernel written in Bass

```python
import math
import jax, jax.numpy as jnp
from jax import lax
import numpy as np

D_MODEL = 2048
BATCH = 8
SEQ = 2048
DEPTH = 2

ALPHA = (2.0 * DEPTH) ** 0.25
BETA = (8.0 * DEPTH) ** -0.25
LN_EPS = 1e-5

HEAD_DIM = 64
N_HEADS = D_MODEL // HEAD_DIM
N_KV = N_HEADS // 8
GROUP = N_HEADS // N_KV
WINDOW = 128
BLOCK = 128
QKV_DIM = (N_HEADS + 2 * N_KV) * HEAD_DIM

POOL_WINDOWS = (2, 4, 8, 16)
N_POOL_GROUPS = len(POOL_WINDOWS)
POOL_GC = D_MODEL // N_POOL_GROUPS

N_KEYS = 128
N_EXPERTS = N_KEYS * N_KEYS
PEER_HEADS = 8
PEER_TOPK = 16
PEER_DK = 128
PEER_CHUNK = 128

N_ATTN_LAYERS = (DEPTH + 1) // 2
N_POOL_LAYERS = DEPTH // 2

kernel_name = "hybrid_swa_pool_peer_deepnorm"


def layer_norm(x, g, b):
    xf = x.astype(jnp.float32)
    mu = jnp.mean(xf, axis=-1, keepdims=True)
    var = jnp.mean(jnp.square(xf - mu), axis=-1, keepdims=True)
    y = (xf - mu) * lax.rsqrt(var + LN_EPS)
    return (y * g.astype(jnp.float32) + b.astype(jnp.float32)).astype(x.dtype)


def alibi_slopes(n):
    return np.array([2.0 ** (-8.0 * (h + 1) / n) for h in range(n)], dtype=np.float32)


def swa_attention(x, w_qkv, w_o, sinks):
    B, S, _ = x.shape
    nb = S // BLOCK
    qkv = x @ w_qkv
    q = qkv[..., : N_HEADS * HEAD_DIM]
    k = qkv[..., N_HEADS * HEAD_DIM:(N_HEADS + N_KV) * HEAD_DIM]
    v = qkv[..., (N_HEADS + N_KV) * HEAD_DIM:]
    q = q.reshape(B, nb, BLOCK, N_KV, GROUP, HEAD_DIM)
    k = k.reshape(B, nb, BLOCK, N_KV, HEAD_DIM)
    v = v.reshape(B, nb, BLOCK, N_KV, HEAD_DIM)
    pad = ((0, 0), (1, 0), (0, 0), (0, 0), (0, 0))
    kb = jnp.concatenate([jnp.pad(k, pad)[:, :-1], k], axis=2)
    vb = jnp.concatenate([jnp.pad(v, pad)[:, :-1], v], axis=2)

    scores = jnp.einsum('bnqkgd,bnskd->bnkgqs', q, kb).astype(jnp.float32)
    scores = scores * (1.0 / math.sqrt(HEAD_DIM))

    dist = (np.arange(BLOCK)[:, None] + BLOCK) - np.arange(2 * BLOCK)[None, :]
    band = (dist >= 0) & (dist < WINDOW)
    s_abs = (np.arange(nb)[:, None] - 1) * BLOCK + np.arange(2 * BLOCK)[None, :]
    mask = band[None] & (s_abs >= 0)[:, None, :]
    slopes = jnp.asarray(alibi_slopes(N_HEADS).reshape(N_KV, GROUP))
    alibi = -slopes[:, :, None, None] * jnp.asarray(dist.astype(np.float32))

    scores = jnp.where(mask[None, :, None, None], scores + alibi[None, None], -1e30)
    sink = sinks.astype(jnp.float32).reshape(N_KV, GROUP)[None, None, :, :, None, None]
    m = jnp.maximum(jnp.max(scores, axis=-1, keepdims=True), sink)
    e = jnp.exp(scores - m)
    p = e / (jnp.sum(e, axis=-1, keepdims=True) + jnp.exp(sink - m))

    out = jnp.einsum('bnkgqs,bnskd->bnqkgd', p.astype(vb.dtype), vb)
    out = out.reshape(B, S, N_HEADS * HEAD_DIM)
    return out @ w_o


def multiscale_pool(x, w_pool, scale):
    B, S, D = x.shape
    t = np.arange(S)
    c = jnp.cumsum(x.astype(jnp.float32), axis=1)
    c = jnp.pad(c, ((0, 0), (1, 0), (0, 0)))
    pooled = []
    for g, w in enumerate(POOL_WINDOWS):
        cg = c[..., g * POOL_GC:(g + 1) * POOL_GC]
        lo_idx = np.maximum(t + 1 - w, 0)
        cnt = np.minimum(t + 1, w).astype(np.float32)
        mean = (cg[:, 1:] - cg[:, lo_idx]) / jnp.asarray(cnt)[None, :, None]
        pooled.append(mean)
    pooled = jnp.stack(pooled, axis=2)
    mix = (pooled - x.astype(jnp.float32).reshape(B, S, N_POOL_GROUPS, POOL_GC)).astype(x.dtype)
    y = jnp.einsum('bsgc,gcd->bsgd', mix, w_pool).reshape(B, S, D)
    return y * scale


def peer(x2d, w_query, sub_keys, u_tab, v_tab):
    T, D = x2d.shape
    xc = x2d.reshape(T // PEER_CHUNK, PEER_CHUNK, D)

    def chunk(xb):
        C = xb.shape[0]
        q = (xb @ w_query).reshape(C, PEER_HEADS, 2, PEER_DK)
        s = jnp.einsum('chpd,hpnd->chpn', q, sub_keys).astype(jnp.float32)
        sv, si = lax.top_k(s, PEER_TOPK)
        cand = sv[:, :, 0, :, None] + sv[:, :, 1, None, :]
        cand_idx = si[:, :, 0, :, None] * N_KEYS + si[:, :, 1, None, :]
        cand = cand.reshape(C, PEER_HEADS, PEER_TOPK * PEER_TOPK)
        cand_idx = cand_idx.reshape(C, PEER_HEADS, PEER_TOPK * PEER_TOPK)
        best, pos = lax.top_k(cand, PEER_TOPK)
        idx = jnp.take_along_axis(cand_idx, pos, axis=-1)
        gate = jax.nn.softmax(best, axis=-1).astype(xb.dtype)
        u = u_tab[idx]
        h = jnp.einsum('chkd,cd->chk', u, xb)
        a = gate * jax.nn.gelu(h, approximate=False)
        v = v_tab[idx]
        return jnp.einsum('chk,chkd->cd', a, v)

    return lax.map(chunk, xc).reshape(T, D)


def setup_inputs(seed: int = 0) -> dict:
    key = jax.random.key(seed)
    ks = jax.random.split(key, 14)
    f32 = jnp.float32
    x = jax.random.normal(ks[0], (BATCH, SEQ, D_MODEL), f32)

    w_qkv = jax.random.normal(ks[1], (N_ATTN_LAYERS, D_MODEL, QKV_DIM), f32) * D_MODEL ** -0.5
    col_scale = jnp.concatenate([jnp.ones(((N_HEADS + N_KV) * HEAD_DIM,), f32),
                                 jnp.full((N_KV * HEAD_DIM,), BETA, f32)])
    w_qkv = w_qkv * col_scale
    w_o = jax.random.normal(ks[2], (N_ATTN_LAYERS, N_HEADS * HEAD_DIM, D_MODEL), f32) * (
        (N_HEADS * HEAD_DIM) ** -0.5 * BETA)
    sinks = jax.random.normal(ks[3], (N_ATTN_LAYERS, N_HEADS), f32) * 0.5

    pool_w = jax.random.normal(ks[4], (N_POOL_LAYERS, N_POOL_GROUPS, POOL_GC, POOL_GC), f32) * (
        POOL_GC ** -0.5 * BETA)
    pool_scale = 1.0 + 0.02 * jax.random.normal(ks[5], (N_POOL_LAYERS, D_MODEL), f32)

    ln_gain = 1.0 + 0.02 * jax.random.normal(ks[6], (DEPTH, 2, D_MODEL), f32)
    ln_bias = 0.02 * jax.random.normal(ks[7], (DEPTH, 2, D_MODEL), f32)

    peer_w_query = jax.random.normal(ks[8], (DEPTH, D_MODEL, PEER_HEADS * 2 * PEER_DK), f32) * D_MODEL ** -0.5
    peer_sub_keys = jax.random.normal(ks[9], (DEPTH, PEER_HEADS, 2, N_KEYS, PEER_DK), f32) * PEER_DK ** -0.5
    peer_u = jax.random.normal(ks[10], (DEPTH, N_EXPERTS, D_MODEL), f32) * D_MODEL ** -0.5
    peer_v = jax.random.normal(ks[11], (DEPTH, N_EXPERTS, D_MODEL), f32) * (PEER_HEADS ** -0.5 * BETA)
    return {"x": x, "attn_w_qkv": w_qkv, "attn_w_o": w_o, "attn_sinks": sinks,
            "pool_w": pool_w, "pool_scale": pool_scale, "ln_gain": ln_gain, "ln_bias": ln_bias,
            "peer_w_query": peer_w_query, "peer_sub_keys": peer_sub_keys,
            "peer_u": peer_u, "peer_v": peer_v}


def reference(x, attn_w_qkv, attn_w_o, attn_sinks, pool_w, pool_scale, ln_gain, ln_bias,
              peer_w_query, peer_sub_keys, peer_u, peer_v):
    B, S, D = x.shape
    for i in range(DEPTH):
        j = i // 2
        if i % 2 == 0:
            mix = swa_attention(x, attn_w_qkv[j], attn_w_o[j], attn_sinks[j])
        else:
            mix = multiscale_pool(x, pool_w[j], pool_scale[j])
        x = layer_norm(ALPHA * x + mix, ln_gain[i, 0], ln_bias[i, 0])
        f = peer(x.reshape(B * S, D), peer_w_query[i], peer_sub_keys[i], peer_u[i], peer_v[i])
        x = layer_norm(ALPHA * x + f.reshape(B, S, D), ln_gain[i, 1], ln_bias[i, 1])
    return x
```

```python
import math
from contextlib import ExitStack

import numpy as np

import concourse.bass as bass
import concourse.mybir as mybir
from concourse.bass_utils import run_bass_kernel_spmd

F32 = mybir.dt.float32
F32R = mybir.dt.float32r
BF16 = mybir.dt.bfloat16
I32 = mybir.dt.int32
U32 = mybir.dt.uint32
AF = mybir.ActivationFunctionType
ALU = mybir.AluOpType
AX = mybir.AxisListType

D_MODEL = 2048
SEQ = 2048
BATCH = 8
DEPTH = 2
ALPHA = (2.0 * DEPTH) ** 0.25
LN_EPS = 1e-5
HEAD_DIM = 64
N_HEADS = 32
N_KV = 4
GROUP = 8
QKV_DIM = 2560
N_KEYS = 128
N_EXPERTS = 16384
PEER_HEADS = 8
PEER_TOPK = 16
POOL_WINDOWS = (2, 4, 8, 16)
P = 128
NEG = -1.0e30

SEM_LIMIT = 30000


class Buf:
    __slots__ = ("name", "w", "r")

    def __init__(self, name=""):
        self.name = name
        self.w = None
        self.r = {}


class _Eng:
    def __init__(self, name, h):
        self.name = name
        self.h = h
        self.sem = None
        self.count = 0
        self.nsem = 0
        self.waited = {}


class Kern:
    def __init__(self, nc, stack, n_dma_sems=12):
        self.nc = nc
        self.stack = stack
        self.eng = {
            "pe": _Eng("pe", nc.tensor),
            "act": _Eng("act", nc.scalar),
            "dve": _Eng("dve", nc.vector),
            "pool": _Eng("pool", nc.gpsimd),
            "sp": _Eng("sp", nc.sync),
        }
        self.n_dma_sems = n_dma_sems
        self.dma_sems = {}
        self.dma_rot = {}
        self._semid = 0
        self.all_dma = []

    def _new_sem(self, tag):
        self._semid += 1
        return self.stack.enter_context(self.nc.semaphore(f"s{self._semid}_{tag}"))

    def _wait(self, E, deps, strict=False):
        for d in deps:
            if d is None:
                continue
            sem, val, src, kind = d
            if src == E.name and not strict:
                if kind == "r" or E.name == "pe" or E.name == "sp":
                    continue
            key = id(sem)
            if E.waited.get(key, 0) >= val:
                continue
            E.h.wait_ge(sem, val)
            E.waited[key] = val

    def _deps(self, reads, writes):
        deps = []
        for b in reads:
            if b.w is not None:
                deps.append(b.w + ("w",))
        for b in writes:
            if b.w is not None:
                deps.append(b.w + ("w",))
            for t in b.r.values():
                deps.append(t + ("r",))
        return deps

    def _commit(self, tok, reads, writes):
        for b in writes:
            b.w = tok
            b.r = {}
        for b in reads:
            if b in writes:
                continue
            key = id(tok[0])
            old = b.r.get(key)
            if old is None or old[1] < tok[1]:
                b.r[key] = tok

    def op(self, e, fn, reads=(), writes=()):
        E = self.eng[e]
        self._wait(E, self._deps(reads, writes))
        ins = fn(E.h)
        if E.sem is None or E.count >= SEM_LIMIT:
            E.sem = self._new_sem(E.name)
            E.count = 0
        ins.then_inc(E.sem, 1)
        E.count += 1
        tok = (E.sem, E.count, E.name)
        self._commit(tok, reads, writes)
        return tok

    def dma(self, q, fn, reads=(), writes=()):
        E = self.eng[q]
        self._wait(E, self._deps(reads, writes), strict=True)
        if q not in self.dma_sems:
            self.dma_sems[q] = []
            self.dma_rot[q] = 0
        pool = self.dma_sems[q]
        i = self.dma_rot[q]
        self.dma_rot[q] = (i + 1) % self.n_dma_sems
        if i >= len(pool):
            ent = [self._new_sem("d" + q), 0]
            pool.append(ent)
            self.all_dma.append(ent)
        ent = pool[i]
        if ent[1] >= SEM_LIMIT:
            self._wait(E, [(ent[0], ent[1], "dma", "w")])
            ent = [self._new_sem("d" + q), 0]
            pool[i] = ent
            self.all_dma.append(ent)
        if ent[1] > 0:
            self._wait(E, [(ent[0], ent[1], "dma", "w")])
        ins = fn(E.h)
        ins.then_inc(ent[0], 16)
        ent[1] += 16
        tok = (ent[0], ent[1], "dma")
        self._commit(tok, reads, writes)
        return tok

    def barrier(self):
        toks = []
        for E in self.eng.values():
            if E.sem is not None and E.count > 0:
                toks.append((E.sem, E.count, E.name, "w"))
        for ent in self.all_dma:
            if ent[1] > 0:
                toks.append((ent[0], ent[1], "dma", "w"))
        for E in self.eng.values():
            self._wait(E, [t for t in toks if t[2] != E.name])

    def finish(self, final_toks):
        E = self.eng["sp"]
        self._wait(E, [t + ("w",) for t in final_toks])


class Ctx:
    def __init__(self, nc, K, tag):
        self.nc = nc
        self.K = K
        self.tag = tag
        self.st = ExitStack()
        self.n = 0

    def sb(self, shape, dt, name="t"):
        self.n += 1
        return self.st.enter_context(self.nc.sbuf_tensor(f"{self.tag}_{name}{self.n}", list(shape), dt))

    def ps(self, shape, dt, name="p"):
        self.n += 1
        return self.st.enter_context(self.nc.psum_tensor(f"{self.tag}_{name}{self.n}", list(shape), dt))

    def close(self):
        self.K.barrier()
        self.st.close()


def bcast_mid(ap2d, n):
    return ap2d.unsqueeze(1).to_broadcast([ap2d.shape[0], n, ap2d.shape[1]])


def bcast_last(ap2d, n):
    return ap2d.unsqueeze(2).to_broadcast([ap2d.shape[0], ap2d.shape[1], n])


def layer_norm_tile(K, C, y, by, gain, bgain, bias, bbias, out, bout, scr):
    stt, bst, mv, bmv, rs, brs, nm, bnm = scr
    for q in range(4):
        K.op("dve", lambda e: e.bn_stats(out=stt[:, q, :], in_=y[:, q * 512:(q + 1) * 512]),
             reads=[by], writes=[bst])
    K.op("dve", lambda e: e.bn_aggr(out=mv[:], in_=stt[:].rearrange("p a b -> p (a b)")),
         reads=[bst], writes=[bmv])
    K.op("dve", lambda e: e.tensor_scalar(out=rs[:], in0=mv[:, 1:2], scalar1=LN_EPS, scalar2=None, op0=ALU.add),
         reads=[bmv], writes=[brs])
    K.op("act", lambda e: e.activation(out=rs[:], in_=rs[:], func=AF.Sqrt), reads=[brs], writes=[brs])
    K.op("dve", lambda e: e.reciprocal(out=rs[:], in_=rs[:]), reads=[brs], writes=[brs])
    K.op("dve", lambda e: e.scalar_tensor_tensor(out=nm[:], in0=mv[:, 0:1], scalar=-1.0, in1=rs[:],
                                                  op0=ALU.mult, op1=ALU.mult),
         reads=[bmv, brs], writes=[bnm])
    K.op("act", lambda e: e.activation(out=y[:], in_=y[:], func=AF.Identity, scale=rs[:], bias=nm[:]),
         reads=[by, brs, bnm], writes=[by])
    K.op("dve", lambda e: e.tensor_tensor(out=y[:], in0=y[:], in1=gain[:], op=ALU.mult),
         reads=[by, bgain], writes=[by])
    K.op("dve", lambda e: e.tensor_tensor(out=out[:], in0=y[:], in1=bias[:], op=ALU.add),
         reads=[by, bbias], writes=[bout])


def ln_scratch(C):
    stt = C.sb([P, 4, 6], F32, "stt")
    mv = C.sb([P, 2], F32, "mv")
    rs = C.sb([P, 1], F32, "rs")
    nm = C.sb([P, 1], F32, "nm")
    return (stt, Buf(), mv, Buf(), rs, Buf(), nm, Buf())


def load_rep(K, C, dram_row, name):
    t = C.sb([P, D_MODEL], F32, name)
    b = Buf()
    K.dma("sp", lambda e: e.dma_start(out=t[:], in_=dram_row.partition_broadcast(P)), writes=[b])
    return t, b


def peer_phase(nc, K, tag, NT, xin, xout, wq_d, keysT_d, u_tab, v_tab, gain_d, bias_d, cst_d, NB=4, dbg=None):
    C = Ctx(nc, K, tag)
    wq = C.sb([P, 16, D_MODEL], BF16, "wq"); bwq = Buf()
    keysT = C.sb([P, 16, P], BF16, "keysT"); bkeys = Buf()
    cst = C.sb([P, 144], F32, "cst"); bcst = Buf()
    ident = cst[:, 0:128]
    iota16 = cst[:, 128:144]
    K.dma("sp", lambda e: e.dma_start(out=cst[:], in_=cst_d), writes=[bcst])
    wq_v = wq_d.rearrange("(k p) n -> k p n", p=P)
    for k in range(16):
        K.dma("pool", lambda e: e.dma_start(out=wq[:, k, :], in_=wq_v[k]), writes=[bwq])
    K.dma("pool", lambda e: e.dma_start(out=keysT[:], in_=keysT_d), writes=[bkeys])
    gain, bgain = load_rep(K, C, gain_d, "gain")
    bias, bbias = load_rep(K, C, bias_d, "bias")

    xt = [C.sb([P, D_MODEL], F32, "xt") for _ in range(2)]; bxt = [Buf(), Buf()]
    xT = C.sb([P, 16, P], BF16, "xT"); bxT = Buf()
    qT = C.sb([P, 16, P], BF16, "qT"); bqT = Buf()
    sc = C.sb([P, 16, P], F32, "sc"); bsc = Buf()
    wk = [C.sb([P, P], F32, "wk") for _ in range(2)]; bwk = [Buf(), Buf()]
    sv = C.sb([P, 16, 16], F32, "sv"); bsv = Buf()
    si = C.sb([P, 16, 16], U32, "si"); bsi = Buf()
    sif = C.sb([P, 16, 16], F32, "sif"); bsif = Buf()
    cand = C.sb([P, 8, 256], F32, "cand"); bcand = Buf()
    cw = [C.sb([P, 256], F32, "cw") for _ in range(2)]; bcw = [Buf(), Buf()]
    best = C.sb([P, 8, 16], F32, "best"); bbest = Buf()
    pos = C.sb([P, 8, 16], U32, "pos"); bpos = Buf()
    ai = C.sb([P, 128], U32, "ai"); bai = Buf()
    bi = C.sb([P, 128], U32, "bi"); bbi = Buf()
    af = C.sb([P, 128], F32, "af"); baf = Buf()
    bf = C.sb([P, 128], F32, "bf"); bbf = Buf()
    oh = C.sb([P, 128, 16], F32, "oh"); boh = Buf()
    sel0 = C.sb([P, 128], F32, "sel0"); bsel0 = Buf()
    sel1 = C.sb([P, 128], F32, "sel1"); bsel1 = Buf()
    idx = [C.sb([P, 128], I32, "idx") for _ in range(2)]; bidx = [Buf(), Buf()]
    gate = [C.sb([P, 8, 16], F32, "gate") for _ in range(2)]; bgate = [Buf(), Buf()]
    gs = C.sb([P, 8], F32, "gs"); bgs = Buf()
    hb = C.sb([P, 128], F32, "hb"); bhb = Buf()
    ab = C.sb([P, 128], F32, "ab"); bab = Buf()
    gb = [C.sb([P, D_MODEL], F32R, "gb") for _ in range(NB)]; bgb = [Buf() for _ in range(NB)]
    junk = C.sb([P, D_MODEL], BF16, "junk"); bjunk = Buf()
    dg = [C.sb([P, P], F32R, "dg") for _ in range(4)]; bdg = [Buf() for _ in range(4)]
    y = C.sb([P, D_MODEL], F32, "y"); by = Buf()
    ot = C.sb([P, D_MODEL], F32, "ot"); bot = Buf()
    lns = ln_scratch(C)

    psA = C.ps([P, 2, 512], F32, "psA"); bpsA = [Buf(), Buf()]
    psS = C.ps([P, 2, 512], F32, "psS"); bpsS = [Buf(), Buf()]
    psV = C.ps([P, 4, 512], F32, "psV"); bpsV = Buf()

    state = {"gi": 0, "bank": 0}

    def route(n):
        x = xt[n % 2]; bx = bxt[n % 2]
        K.dma("sp", lambda e: e.dma_start(out=x[:], in_=xin[n * P:(n + 1) * P, :]), writes=[bx])
        for g4 in range(4):
            b = state["bank"]; state["bank"] ^= 1
            for j in range(4):
                k = g4 * 4 + j
                K.op("pe", lambda e: e.transpose(out=psA[:, b, j * P:(j + 1) * P], in_=x[:, k * P:(k + 1) * P],
                                                 identity=ident), reads=[bx, bcst], writes=[bpsA[b]])
            K.op("act", lambda e: e.activation(out=xT[:, g4 * 4:(g4 + 1) * 4, :].rearrange("p a b -> p (a b)"),
                                               in_=psA[:, b, :], func=AF.Copy), reads=[bpsA[b]], writes=[bxT])
        for g4 in range(4):
            b = state["bank"]; state["bank"] ^= 1
            for j in range(4):
                hp = g4 * 4 + j
                for k in range(16):
                    K.op("pe", lambda e: e.matmul(out=psA[:, b, j * P:(j + 1) * P], lhsT=wq[:, k, hp * P:(hp + 1) * P],
                                                  rhs=xT[:, k, :], start=(k == 0), stop=(k == 15)),
                         reads=[bwq, bxT], writes=[bpsA[b]])
            K.op("act", lambda e: e.activation(out=qT[:, g4 * 4:(g4 + 1) * 4, :].rearrange("p a b -> p (a b)"),
                                               in_=psA[:, b, :], func=AF.Copy), reads=[bpsA[b]], writes=[bqT])
        for g4 in range(4):
            b = g4 % 2
            for j in range(4):
                hp = g4 * 4 + j
                K.op("pe", lambda e: e.matmul(out=psS[:, b, j * P:(j + 1) * P], lhsT=qT[:, hp, :], rhs=keysT[:, hp, :],
                                              start=True, stop=True), reads=[bqT, bkeys], writes=[bpsS[b]])
            K.op("act", lambda e: e.activation(out=sc[:, g4 * 4:(g4 + 1) * 4, :].rearrange("p a b -> p (a b)"),
                                               in_=psS[:, b, :], func=AF.Copy), reads=[bpsS[b]], writes=[bsc])
        for hp in range(16):
            w = wk[hp % 2]; bw = bwk[hp % 2]
            K.op("dve", lambda e: e.max(out=sv[:, hp, 0:8], in_=sc[:, hp, :]), reads=[bsc], writes=[bsv])
            K.op("dve", lambda e: e.max_index(out=si[:, hp, 0:8], in_max=sv[:, hp, 0:8], in_values=sc[:, hp, :]),
                 reads=[bsc, bsv], writes=[bsi])
            K.op("dve", lambda e: e.match_replace(out=w[:], in_to_replace=sv[:, hp, 0:8], in_values=sc[:, hp, :],
                                                  imm_value=NEG), reads=[bsc, bsv], writes=[bw])
            K.op("dve", lambda e: e.max(out=sv[:, hp, 8:16], in_=w[:]), reads=[bw], writes=[bsv])
            K.op("dve", lambda e: e.max_index(out=si[:, hp, 8:16], in_max=sv[:, hp, 8:16], in_values=w[:]),
                 reads=[bw, bsv], writes=[bsi])
        K.op("dve", lambda e: e.tensor_copy(out=sif[:], in_=si[:]), reads=[bsi], writes=[bsif])
        sv4 = sv[:].rearrange("p (h t) k -> p h t k", t=2)
        for h in range(8):
            K.op("dve", lambda e: e.tensor_tensor(out=cand[:, h, :].rearrange("p (a b) -> p a b", b=16),
                                                  in0=bcast_last(sv4[:, h, 0, :], 16), in1=bcast_mid(sv4[:, h, 1, :], 16),
                                                  op=ALU.add), reads=[bsv], writes=[bcand])
        for h in range(8):
            w = cw[h % 2]; bw = bcw[h % 2]
            K.op("dve", lambda e: e.max(out=best[:, h, 0:8], in_=cand[:, h, :]), reads=[bcand], writes=[bbest])
            K.op("dve", lambda e: e.max_index(out=pos[:, h, 0:8], in_max=best[:, h, 0:8], in_values=cand[:, h, :]),
                 reads=[bcand, bbest], writes=[bpos])
            K.op("dve", lambda e: e.match_replace(out=w[:], in_to_replace=best[:, h, 0:8], in_values=cand[:, h, :],
                                                  imm_value=NEG), reads=[bcand, bbest], writes=[bw])
            K.op("dve", lambda e: e.max(out=best[:, h, 8:16], in_=w[:]), reads=[bw], writes=[bbest])
            K.op("dve", lambda e: e.max_index(out=pos[:, h, 8:16], in_max=best[:, h, 8:16], in_values=w[:]),
                 reads=[bw, bbest], writes=[bpos])
        posf = pos[:].rearrange("p h k -> p (h k)")
        K.op("dve", lambda e: e.tensor_scalar(out=ai[:], in0=posf, scalar1=4, scalar2=None, op0=ALU.logical_shift_right),
             reads=[bpos], writes=[bai])
        K.op("dve", lambda e: e.tensor_scalar(out=bi[:], in0=posf, scalar1=15, scalar2=None, op0=ALU.bitwise_and),
             reads=[bpos], writes=[bbi])
        K.op("dve", lambda e: e.tensor_copy(out=af[:], in_=ai[:]), reads=[bai], writes=[baf])
        K.op("dve", lambda e: e.tensor_copy(out=bf[:], in_=bi[:]), reads=[bbi], writes=[bbf])
        sif4 = sif[:].rearrange("p (h t) k -> p h t k", t=2)
        for (srcf, bsrc, t, sel, bsel) in ((af, baf, 0, sel0, bsel0), (bf, bbf, 1, sel1, bsel1)):
            K.op("dve", lambda e: e.tensor_tensor(out=oh[:], in0=bcast_mid(iota16, 128), in1=bcast_last(srcf[:], 16),
                                                  op=ALU.is_equal), reads=[bcst, bsrc], writes=[boh])
            for h in range(8):
                K.op("dve", lambda e: e.tensor_tensor(out=oh[:, h * 16:(h + 1) * 16, :], in0=oh[:, h * 16:(h + 1) * 16, :],
                                                      in1=bcast_mid(sif4[:, h, t, :], 16), op=ALU.mult),
                     reads=[boh, bsif], writes=[boh])
            K.op("dve", lambda e: e.tensor_reduce(out=sel[:], in_=oh[:], axis=AX.X, op=ALU.add), reads=[boh], writes=[bsel])
        ix = idx[n % 2]; bix = bidx[n % 2]
        K.op("dve", lambda e: e.scalar_tensor_tensor(out=sel0[:], in0=sel0[:], scalar=128.0, in1=sel1[:],
                                                      op0=ALU.mult, op1=ALU.add), reads=[bsel0, bsel1], writes=[bsel0])
        K.op("dve", lambda e: e.tensor_copy(out=ix[:], in_=sel0[:]), reads=[bsel0], writes=[bix])
        g = gate[n % 2]; bg = bgate[n % 2]
        K.op("dve", lambda e: e.tensor_tensor(out=g[:], in0=best[:], in1=best[:, :, 0:1].to_broadcast([P, 8, 16]),
                                              op=ALU.subtract), reads=[bbest], writes=[bg])
        K.op("act", lambda e: e.activation(out=g[:], in_=g[:], func=AF.Exp), reads=[bg], writes=[bg])
        K.op("dve", lambda e: e.tensor_reduce(out=gs[:], in_=g[:], axis=AX.X, op=ALU.add), reads=[bg], writes=[bgs])
        K.op("dve", lambda e: e.reciprocal(out=gs[:], in_=gs[:]), reads=[bgs], writes=[bgs])
        K.op("dve", lambda e: e.tensor_tensor(out=g[:], in0=g[:], in1=bcast_last(gs[:], 16), op=ALU.mult),
             reads=[bg, bgs], writes=[bg])

    def experts(n):
        x = xt[n % 2]; bx = bxt[n % 2]
        ix = idx[n % 2]; bix = bidx[n % 2]
        g = gate[n % 2]; bg = bgate[n % 2]
        for s in range(128):
            j = state["gi"] % NB; state["gi"] += 1
            K.dma("pool", lambda e: e.indirect_dma_start(out=gb[j][:, :], out_offset=None, in_=u_tab,
                                                         in_offset=bass.IndirectOffsetOnAxis(ap=ix[:, s:s + 1], axis=0)),
                  reads=[bix], writes=[bgb[j]])
            K.op("dve", lambda e: e.scalar_tensor_tensor(out=junk[:], in0=gb[j][:].bitcast(F32), scalar=1.0, in1=x[:],
                                                          op0=ALU.mult, op1=ALU.mult, accum_out=hb[:, s:s + 1]),
                 reads=[bgb[j], bx], writes=[bjunk, bhb])
        K.op("act", lambda e: e.activation(out=ab[:], in_=hb[:], func=AF.Gelu), reads=[bhb], writes=[bab])
        K.op("dve", lambda e: e.tensor_tensor(out=ab[:], in0=ab[:], in1=g[:].rearrange("p h k -> p (h k)"), op=ALU.mult),
             reads=[bab, bg], writes=[bab])
        if dbg is not None:
            K.dma("sp", lambda e: e.dma_start(out=dbg["idx"][n * P:(n + 1) * P, :], in_=ix[:]), reads=[bix])
            K.dma("sp", lambda e: e.dma_start(out=dbg["a"][n * P:(n + 1) * P, :], in_=ab[:]), reads=[bab])
            K.dma("sp", lambda e: e.dma_start(out=dbg["h"][n * P:(n + 1) * P, :], in_=hb[:]), reads=[bhb])
        for s in range(128):
            j = state["gi"] % NB; state["gi"] += 1
            d = dg[s % 4]; bd = bdg[s % 4]
            K.dma("pool", lambda e: e.indirect_dma_start(out=gb[j][:, :], out_offset=None, in_=v_tab,
                                                         in_offset=bass.IndirectOffsetOnAxis(ap=ix[:, s:s + 1], axis=0)),
                  reads=[bix], writes=[bgb[j]])
            K.op("dve", lambda e: e.tensor_scalar(out=d[:], in0=ident, scalar1=ab[:, s:s + 1], scalar2=None, op0=ALU.mult),
                 reads=[bcst, bab], writes=[bd])
            for q in range(4):
                K.op("pe", lambda e: e.matmul(out=psV[:, q, :], lhsT=d[:], rhs=gb[j][:, q * 512:(q + 1) * 512],
                                              start=(s == 0), stop=(s == 127)), reads=[bd, bgb[j]], writes=[bpsV])
        K.op("dve", lambda e: e.scalar_tensor_tensor(out=y[:], in0=x[:], scalar=ALPHA, in1=psV[:].rearrange("p a b -> p (a b)"),
                                                      op0=ALU.mult, op1=ALU.add), reads=[bx, bpsV], writes=[by])
        layer_norm_tile(K, C, y, by, gain, bgain, bias, bbias, ot, bot, lns)
        return K.dma("sp", lambda e: e.dma_start(out=xout[n * P:(n + 1) * P, :], in_=ot[:]), reads=[bot])

    toks = []
    route(0)
    for n in range(NT):
        if n + 1 < NT:
            route(n + 1)
        toks.append(experts(n))
    C.close()
    return toks


def const_tables():
    cst = np.zeros((P, 144), np.float32)
    cst[:, :128] = np.eye(128, dtype=np.float32)
    cst[:, 128:144] = np.arange(16, dtype=np.float32)[None, :]
    slopes = np.array([2.0 ** (-8.0 * (h + 1) / N_HEADS) for h in range(N_HEADS)], dtype=np.float32)
    s = np.arange(P)[:, None]
    q = np.arange(P)[None, :]
    ab = np.zeros((P, 2, N_HEADS, P), np.float32)
    dist0 = (q + 128 - s).astype(np.float32)
    dist1 = (q - s).astype(np.float32)
    for h in range(N_HEADS):
        ab[:, 0, h, :] = np.where(s > q, -slopes[h] * dist0, -30000.0)
        ab[:, 1, h, :] = np.where(q >= s, -slopes[h] * dist1, -30000.0)
    pm = np.zeros((12, P, P), np.float32)
    for g, w in enumerate(POOL_WINDOWS):
        for t in range(P):
            for sidx in range(max(0, t - w + 1), t + 1):
                pm[g, sidx, t] += 1.0 / w
            pm[g, t, t] -= 1.0
            for sp_ in range(P):
                if sp_ - 128 >= t - w + 1:
                    pm[4 + g, sp_, t] += 1.0 / w
            cnt = min(t + 1, w)
            for sidx in range(max(0, t - w + 1), t + 1):
                pm[8 + g, sidx, t] += np.float32(1.0) / np.float32(cnt)
            pm[8 + g, t, t] -= 1.0
    return cst, ab, pm


def pool_phase(nc, K, tag, NT, xin, xout, poolw_d, scale_d, gain_d, bias_d, pm_d):
    C = Ctx(nc, K, tag)
    wp = C.sb([P, 4, 4, 512], BF16, "wp"); bwp = Buf()
    for g in range(4):
        K.dma("pool", lambda e: e.dma_start(out=wp[:, g, :, :], in_=poolw_d[g].rearrange("(cc p) n -> p cc n", p=P)),
              writes=[bwp])
    pm = C.sb([P, 12, P], F32, "pm"); bpm = Buf()
    K.dma("sp", lambda e: e.dma_start(out=pm[:], in_=pm_d.rearrange("m s t -> s m t")), writes=[bpm])
    gain, bgain = load_rep(K, C, gain_d, "gain")
    bias, bbias = load_rep(K, C, bias_d, "bias")
    scl, bscl = load_rep(K, C, scale_d, "scl")
    xt = [C.sb([P, D_MODEL], F32, "xt") for _ in range(2)]; bxt = [Buf(), Buf()]
    mixT = C.sb([P, 16, P], BF16, "mixT"); bmixT = Buf()
    y = C.sb([P, D_MODEL], F32, "y"); by = Buf()
    ot = C.sb([P, D_MODEL], F32, "ot"); bot = Buf()
    lns = ln_scratch(C)
    psM = C.ps([P, 4, 512], F32, "psM"); bpsM = [Buf() for _ in range(4)]
    psY = C.ps([P, 4, 512], F32, "psY"); bpsY = Buf()
    toks = []
    for n in range(NT):
        x = xt[n % 2]; bx = bxt[n % 2]
        xp = xt[(n - 1) % 2]; bxp = bxt[(n - 1) % 2]
        K.dma("sp", lambda e: e.dma_start(out=x[:], in_=xin[n * P:(n + 1) * P, :]), writes=[bx])
        for g in range(4):
            mc = pm[:, (8 + g) if n == 0 else g, :]
            mp = pm[:, 4 + g, :]
            for cc in range(4):
                k = g * 4 + cc
                K.op("pe", lambda e: e.matmul(out=psM[:, g, cc * P:(cc + 1) * P], lhsT=x[:, k * P:(k + 1) * P], rhs=mc,
                                              start=True, stop=(n == 0)), reads=[bx, bpm], writes=[bpsM[g]])
                if n > 0:
                    K.op("pe", lambda e: e.matmul(out=psM[:, g, cc * P:(cc + 1) * P], lhsT=xp[:, k * P:(k + 1) * P], rhs=mp,
                                                  start=False, stop=True), reads=[bxp, bpm], writes=[bpsM[g]])
            K.op("act", lambda e: e.activation(out=mixT[:, g * 4:(g + 1) * 4, :].rearrange("p a b -> p (a b)"),
                                               in_=psM[:, g, :], func=AF.Copy), reads=[bpsM[g]], writes=[bmixT])
        for g in range(4):
            for cc in range(4):
                K.op("pe", lambda e: e.matmul(out=psY[:, g, :], lhsT=mixT[:, g * 4 + cc, :], rhs=wp[:, g, cc, :],
                                              start=(cc == 0), stop=(cc == 3)), reads=[bmixT, bwp], writes=[bpsY])
        K.op("dve", lambda e: e.tensor_tensor(out=y[:], in0=psY[:].rearrange("p a b -> p (a b)"), in1=scl[:], op=ALU.mult),
             reads=[bpsY, bscl], writes=[by])
        K.op("dve", lambda e: e.scalar_tensor_tensor(out=y[:], in0=x[:], scalar=ALPHA, in1=y[:], op0=ALU.mult, op1=ALU.add),
             reads=[bx, by], writes=[by])
        layer_norm_tile(K, C, y, by, gain, bgain, bias, bbias, ot, bot, lns)
        toks.append(K.dma("sp", lambda e: e.dma_start(out=xout[n * P:(n + 1) * P, :], in_=ot[:]), reads=[bot]))
    C.close()
    return toks


def attn_phase1(nc, K, tag, NT, xin, attnT_d, wqkv_d, sinks_d, cst_d, abias_d):
    C = Ctx(nc, K, tag)
    wqkv = C.sb([P, 16, QKV_DIM], BF16, "wqkv"); bw = Buf()
    wv = wqkv_d.rearrange("(k p) n -> k p n", p=P)
    for k in range(16):
        for hh in range(2):
            K.dma("pool", lambda e: e.dma_start(out=wqkv[:, k, hh * 1280:(hh + 1) * 1280], in_=wv[k][:, hh * 1280:(hh + 1) * 1280]),
                  writes=[bw])
    cst = C.sb([P, 144], F32, "cst"); bcst = Buf()
    K.dma("sp", lambda e: e.dma_start(out=cst[:], in_=cst_d), writes=[bcst])
    ident = cst[:, 0:128]
    identb = C.sb([P, P], BF16, "identb"); bidb = Buf()
    K.op("dve", lambda e: e.tensor_copy(out=identb[:], in_=ident), reads=[bcst], writes=[bidb])
    abias = C.sb([P, 2, N_HEADS, P], F32, "abias"); bab = Buf()
    K.dma("sp", lambda e: e.dma_start(out=abias[:], in_=abias_d), writes=[bab])
    esink = C.sb([P, N_HEADS], F32, "esink"); bes = Buf()
    K.dma("sp", lambda e: e.dma_start(out=esink[:], in_=sinks_d.partition_broadcast(P)), writes=[bes])
    K.op("act", lambda e: e.activation(out=esink[:], in_=esink[:], func=AF.Exp), reads=[bes], writes=[bes])

    xt = [C.sb([P, D_MODEL], F32, "xt") for _ in range(2)]; bxt = [Buf(), Buf()]
    xT = C.sb([P, 16, P], BF16, "xT"); bxT = Buf()
    QT = C.sb([64, N_HEADS, P], BF16, "QT"); bQT = Buf()
    KT = [C.sb([64, N_KV, P], BF16, "KT") for _ in range(2)]; bKT = [Buf(), Buf()]
    Va = [C.sb([P, N_KV, 65], BF16, "Va") for _ in range(2)]; bVa = [Buf(), Buf()]
    for i in range(2):
        K.op("dve", lambda e: e.memset(Va[i][:], 1.0), writes=[bVa[i]])
    tmp = [C.sb([P, 1024], F32, "tmp") for _ in range(2)]; btmp = [Buf(), Buf()]
    E = [C.sb([P, 8, P], BF16, "E") for _ in range(2)]; bE = [Buf(), Buf()]
    rec = C.sb([P, 2, 4], F32, "rec"); brec = Buf()
    attn = C.sb([P, N_HEADS, 64], BF16, "attn"); battn = Buf()
    aT = [C.sb([P, 16 * P], BF16, "aT") for _ in range(2)]; baT = [Buf(), Buf()]

    psA = C.ps([P, 2, 512], F32, "psA"); bpsA = [Buf(), Buf()]
    psS = C.ps([P, 2, 1024], F32, "psS"); bpsS = [Buf(), Buf()]
    psO = C.ps([P, 2, 512], F32, "psO"); bpsO = Buf()
    bank = [0]

    def nb():
        b = bank[0]; bank[0] ^= 1
        return b

    for n in range(NT):
        x = xt[n % 2]; bx = bxt[n % 2]
        cur = n % 2; prv = (n - 1) % 2
        K.dma("sp", lambda e: e.dma_start(out=x[:], in_=xin[n * P:(n + 1) * P, :]), writes=[bx])
        for g4 in range(4):
            b = nb()
            for j in range(4):
                k = g4 * 4 + j
                K.op("pe", lambda e: e.transpose(out=psA[:, b, j * P:(j + 1) * P], in_=x[:, k * P:(k + 1) * P], identity=ident),
                     reads=[bx, bcst], writes=[bpsA[b]])
            K.op("act", lambda e: e.activation(out=xT[:, g4 * 4:(g4 + 1) * 4, :].rearrange("p a b -> p (a b)"),
                                               in_=psA[:, b, :], func=AF.Copy), reads=[bpsA[b]], writes=[bxT])
        for g4 in range(8):
            b = nb()
            for j in range(4):
                h = g4 * 4 + j
                for k in range(16):
                    K.op("pe", lambda e: e.matmul(out=psA[0:64, b, j * P:(j + 1) * P], lhsT=wqkv[:, k, h * 64:(h + 1) * 64],
                                                  rhs=xT[:, k, :], start=(k == 0), stop=(k == 15)),
                         reads=[bw, bxT], writes=[bpsA[b]])
            K.op("act", lambda e: e.activation(out=QT[:, g4 * 4:(g4 + 1) * 4, :].rearrange("p a b -> p (a b)"),
                                               in_=psA[0:64, b, :], func=AF.Copy), reads=[bpsA[b]], writes=[bQT])
        b = nb()
        for g in range(N_KV):
            for k in range(16):
                K.op("pe", lambda e: e.matmul(out=psA[0:64, b, g * P:(g + 1) * P], lhsT=wqkv[:, k, 2048 + g * 64:2048 + (g + 1) * 64],
                                              rhs=xT[:, k, :], start=(k == 0), stop=(k == 15)),
                     reads=[bw, bxT], writes=[bpsA[b]])
        K.op("act", lambda e: e.activation(out=KT[cur][:].rearrange("p a b -> p (a b)"), in_=psA[0:64, b, :], func=AF.Copy),
             reads=[bpsA[b]], writes=[bKT[cur]])
        b = nb()
        for k in range(16):
            K.op("pe", lambda e: e.matmul(out=psA[:, b, 0:256], lhsT=xT[:, k, :], rhs=wqkv[:, k, 2304:2560],
                                          start=(k == 0), stop=(k == 15)), reads=[bw, bxT], writes=[bpsA[b]])
        K.op("act", lambda e: e.activation(out=Va[cur][:, :, 0:64], in_=psA[:, b, 0:256].rearrange("p (g d) -> p g d", d=64),
                                           func=AF.Copy), reads=[bpsA[b]], writes=[bVa[cur]])
        blks = ([(0, prv)] if n > 0 else []) + [(1, cur)]
        for g in range(N_KV):
            for (blk, buf) in blks:
                for hl in range(8):
                    h = g * 8 + hl
                    K.op("pe", lambda e: e.matmul(out=psS[:, blk, hl * P:(hl + 1) * P], lhsT=KT[buf][:, g, :], rhs=QT[:, h, :],
                                                  start=True, stop=True), reads=[bKT[buf], bQT], writes=[bpsS[blk]])
                K.op("dve", lambda e: e.scalar_tensor_tensor(out=tmp[blk][:], in0=psS[:, blk, :], scalar=0.125,
                                                              in1=abias[:, blk, g * 8:(g + 1) * 8, :].rearrange("p a b -> p (a b)"),
                                                              op0=ALU.mult, op1=ALU.add),
                     reads=[bpsS[blk], bab], writes=[btmp[blk]])
                K.op("act", lambda e: e.activation(out=E[blk][:].rearrange("p a b -> p (a b)"), in_=tmp[blk][:], func=AF.Exp),
                     reads=[btmp[blk]], writes=[bE[blk]])
            for hl in range(8):
                o = psO[:, hl // 4, (hl % 4) * 65:(hl % 4) * 65 + 65]
                for i, (blk, buf) in enumerate(blks):
                    K.op("pe", lambda e: e.matmul(out=o, lhsT=E[blk][:, hl, :], rhs=Va[buf][:, g, :],
                                                  start=(i == 0), stop=(i == len(blks) - 1)),
                         reads=[bE[blk], bVa[buf]], writes=[bpsO])
            ov = psO[:, :, 0:260].rearrange("p b (h e) -> p b h e", e=65)
            K.op("dve", lambda e: e.tensor_tensor(out=rec[:], in0=ov[:, :, :, 64],
                                                  in1=esink[:, g * 8:(g + 1) * 8].rearrange("p (b h) -> p b h", h=4), op=ALU.add),
                 reads=[bpsO, bes], writes=[brec])
            K.op("dve", lambda e: e.reciprocal(out=rec[:], in_=rec[:]), reads=[brec], writes=[brec])
            for bk in range(2):
                K.op("dve", lambda e: e.tensor_tensor(out=attn[:, g * 8 + bk * 4:g * 8 + bk * 4 + 4, :], in0=ov[:, bk, :, 0:64],
                                                      in1=bcast_last(rec[:, bk, :], 64), op=ALU.mult),
                     reads=[bpsO, brec], writes=[battn])
        at = aT[n % 2]; bat = baT[n % 2]
        af2 = attn[:].rearrange("p h d -> p (h d)")
        for g8 in range(2):
            b = nb()
            pb = psA[:, b, :].bitcast(BF16)
            for j in range(8):
                k = g8 * 8 + j
                K.op("pe", lambda e: e.transpose(out=pb[:, j * P:(j + 1) * P], in_=af2[:, k * P:(k + 1) * P], identity=identb[:]),
                     reads=[battn, bidb], writes=[bpsA[b]])
            K.op("act", lambda e: e.activation(out=at[:, g8 * 1024:(g8 + 1) * 1024], in_=pb, func=AF.Copy),
                 reads=[bpsA[b]], writes=[bat])
        K.dma("sp", lambda e: e.dma_start(out=attnT_d[n], in_=at[:]), reads=[bat])
    C.close()


def attn_phase2(nc, K, tag, NT, xin, xout, attnT_d, wo_d, gain_d, bias_d):
    C = Ctx(nc, K, tag)
    wo = C.sb([P, 16, D_MODEL], BF16, "wo"); bwo = Buf()
    wv = wo_d.rearrange("(k p) n -> k p n", p=P)
    for k in range(16):
        K.dma("pool", lambda e: e.dma_start(out=wo[:, k, :], in_=wv[k]), writes=[bwo])
    gain, bgain = load_rep(K, C, gain_d, "gain")
    bias, bbias = load_rep(K, C, bias_d, "bias")
    xt = [C.sb([P, D_MODEL], F32, "xt") for _ in range(2)]; bxt = [Buf(), Buf()]
    aT = [C.sb([P, 16, P], BF16, "aT") for _ in range(2)]; baT = [Buf(), Buf()]
    y = C.sb([P, D_MODEL], F32, "y"); by = Buf()
    ot = C.sb([P, D_MODEL], F32, "ot"); bot = Buf()
    lns = ln_scratch(C)
    psY = [C.ps([P, 4, 512], F32, "psY") for _ in range(2)]; bpsY = [Buf(), Buf()]
    toks = []
    for n in range(NT):
        x = xt[n % 2]; bx = bxt[n % 2]
        a = aT[n % 2]; ba = baT[n % 2]
        py = psY[n % 2]; bpy = bpsY[n % 2]
        K.dma("sp", lambda e: e.dma_start(out=x[:], in_=xin[n * P:(n + 1) * P, :]), writes=[bx])
        K.dma("sp", lambda e: e.dma_start(out=a[:].rearrange("p a b -> p (a b)"), in_=attnT_d[n]), writes=[ba])
        for q in range(4):
            for k in range(16):
                K.op("pe", lambda e: e.matmul(out=py[:, q, :], lhsT=a[:, k, :], rhs=wo[:, k, q * 512:(q + 1) * 512],
                                              start=(k == 0), stop=(k == 15)), reads=[ba, bwo], writes=[bpy])
        K.op("dve", lambda e: e.scalar_tensor_tensor(out=y[:], in0=x[:], scalar=ALPHA, in1=py[:].rearrange("p a b -> p (a b)"),
                                                      op0=ALU.mult, op1=ALU.add), reads=[bx, bpy], writes=[by])
        layer_norm_tile(K, C, y, by, gain, bgain, bias, bbias, ot, bot, lns)
        toks.append(K.dma("sp", lambda e: e.dma_start(out=xout[n * P:(n + 1) * P, :], in_=ot[:]), reads=[bot]))
    C.close()
    return toks


def build_program(NT=16):
    nc = bass.Bass("TRN2", target_bir_lowering=False)
    T = NT * P
    x_d = nc.dram_tensor("x", [T, D_MODEL], F32, kind="ExternalInput")
    wqkv_d = nc.dram_tensor("wqkv", [D_MODEL, QKV_DIM], F32, kind="ExternalInput")
    wo_d = nc.dram_tensor("wo", [D_MODEL, D_MODEL], F32, kind="ExternalInput")
    sinks_d = nc.dram_tensor("sinks", [N_HEADS], F32, kind="ExternalInput")
    poolw_d = nc.dram_tensor("poolw", [4, 512, 512], F32, kind="ExternalInput")
    pscale_d = nc.dram_tensor("pscale", [D_MODEL], F32, kind="ExternalInput")
    gain_d = nc.dram_tensor("lng", [4, D_MODEL], F32, kind="ExternalInput")
    bias_d = nc.dram_tensor("lnb", [4, D_MODEL], F32, kind="ExternalInput")
    wq_d = [nc.dram_tensor(f"wq{l}", [D_MODEL, D_MODEL], F32, kind="ExternalInput") for l in range(2)]
    keys_d = [nc.dram_tensor(f"keysT{l}", [P, 16, P], F32, kind="ExternalInput") for l in range(2)]
    u_d = [nc.dram_tensor(f"u{l}", [N_EXPERTS, D_MODEL], F32, kind="ExternalInput") for l in range(2)]
    v_d = [nc.dram_tensor(f"v{l}", [N_EXPERTS, D_MODEL], F32, kind="ExternalInput") for l in range(2)]
    cst_d = nc.dram_tensor("cst", [P, 144], F32, kind="ExternalInput")
    ab_d = nc.dram_tensor("abias", [P, 2, N_HEADS, P], F32, kind="ExternalInput")
    pm_d = nc.dram_tensor("poolm", [12, P, P], F32, kind="ExternalInput")
    out_d = nc.dram_tensor("out", [T, D_MODEL], F32, kind="ExternalOutput")
    attnT_d = nc.dram_tensor("attnT_scr", [NT, P, 16 * P], BF16, kind="Internal")
    xa_d = nc.dram_tensor("xa_scr", [T, D_MODEL], F32, kind="Internal")
    xb_d = nc.dram_tensor("xb_scr", [T, D_MODEL], F32, kind="Internal")
    xc_d = nc.dram_tensor("xc_scr", [T, D_MODEL], F32, kind="Internal")
    with ExitStack() as st:
        K = Kern(nc, st)
        attn_phase1(nc, K, "a1", NT, x_d.ap(), attnT_d.ap(), wqkv_d.ap(), sinks_d.ap(), cst_d.ap(), ab_d.ap())
        attn_phase2(nc, K, "a2", NT, x_d.ap(), xa_d.ap(), attnT_d.ap(), wo_d.ap(), gain_d[0, :], bias_d[0, :])
        peer_phase(nc, K, "p0", NT, xa_d.ap(), xb_d.ap(), wq_d[0].ap(), keys_d[0].ap(), u_d[0].ap(), v_d[0].ap(),
                   gain_d[1, :], bias_d[1, :], cst_d.ap())
        pool_phase(nc, K, "c1", NT, xb_d.ap(), xc_d.ap(), poolw_d.ap(), pscale_d.ap(), gain_d[2, :], bias_d[2, :], pm_d.ap())
        toks = peer_phase(nc, K, "p1", NT, xc_d.ap(), out_d.ap(), wq_d[1].ap(), keys_d[1].ap(), u_d[1].ap(), v_d[1].ap(),
                          gain_d[3, :], bias_d[3, :], cst_d.ap())
        K.finish(toks)
    return nc


def kernel(x, attn_w_qkv, attn_w_o, attn_sinks, pool_w, pool_scale, ln_gain, ln_bias,
           peer_w_query, peer_sub_keys, peer_u, peer_v):
    f32 = np.float32
    x = np.asarray(x, f32)
    cst, ab, pm = const_tables()
    shared = {
        "wqkv": np.ascontiguousarray(np.asarray(attn_w_qkv, f32)[0]),
        "wo": np.ascontiguousarray(np.asarray(attn_w_o, f32)[0]),
        "sinks": np.ascontiguousarray(np.asarray(attn_sinks, f32)[0]),
        "poolw": np.ascontiguousarray(np.asarray(pool_w, f32)[0]),
        "pscale": np.ascontiguousarray(np.asarray(pool_scale, f32)[0]),
        "lng": np.ascontiguousarray(np.asarray(ln_gain, f32).reshape(4, D_MODEL)),
        "lnb": np.ascontiguousarray(np.asarray(ln_bias, f32).reshape(4, D_MODEL)),
        "cst": cst, "abias": ab, "poolm": pm,
    }
    sk = np.asarray(peer_sub_keys, f32)
    for l in range(2):
        shared[f"wq{l}"] = np.ascontiguousarray(np.asarray(peer_w_query, f32)[l])
        shared[f"keysT{l}"] = np.ascontiguousarray(sk[l].transpose(3, 0, 1, 2).reshape(P, 16, P))
        shared[f"u{l}"] = np.ascontiguousarray(np.asarray(peer_u, f32)[l])
        shared[f"v{l}"] = np.ascontiguousarray(np.asarray(peer_v, f32)[l])
    nc = build_program(16)
    in_maps = []
    for b in range(BATCH):
        m = dict(shared)
        m["x"] = np.ascontiguousarray(x[b])
        in_maps.append(m)
    res = run_bass_kernel_spmd(nc, in_maps, core_ids=list(range(BATCH)))
    return np.stack([np.asarray(r["out"], f32) for r in res.results], axis=0)
```

```python
import math
from contextlib import ExitStack

import numpy as np

import concourse.bass as bass
import concourse.mybir as mybir
from concourse.bass_utils import run_bass_kernel_spmd

F32 = mybir.dt.float32
F32R = mybir.dt.float32r
BF16 = mybir.dt.bfloat16
I32 = mybir.dt.int32
U32 = mybir.dt.uint32
AF = mybir.ActivationFunctionType
ALU = mybir.AluOpType
AX = mybir.AxisListType

D_MODEL = 2048
SEQ = 2048
BATCH = 8
DEPTH = 2
ALPHA = (2.0 * DEPTH) ** 0.25
LN_EPS = 1e-5
HEAD_DIM = 64
N_HEADS = 32
N_KV = 4
GROUP = 8
QKV_DIM = 2560
N_KEYS = 128
N_EXPERTS = 16384
PEER_HEADS = 8
PEER_TOPK = 16
POOL_WINDOWS = (2, 4, 8, 16)
P = 128
NEG = -1.0e30

SEM_LIMIT = 30000


class Buf:
    __slots__ = ("name", "w", "r")

    def __init__(self, name=""):
        self.name = name
        self.w = None
        self.r = {}


class _Eng:
    def __init__(self, name, h):
        self.name = name
        self.h = h
        self.sem = None
        self.count = 0
        self.nsem = 0
        self.waited = {}


class Kern:
    def __init__(self, nc, stack, n_dma_sems=12):
        self.nc = nc
        self.stack = stack
        self.eng = {
            "pe": _Eng("pe", nc.tensor),
            "act": _Eng("act", nc.scalar),
            "dve": _Eng("dve", nc.vector),
            "pool": _Eng("pool", nc.gpsimd),
            "sp": _Eng("sp", nc.sync),
        }
        self.n_dma_sems = n_dma_sems
        self.dma_sems = {}
        self.dma_rot = {}
        self._semid = 0
        self.all_dma = []

    def _new_sem(self, tag):
        self._semid += 1
        return self.stack.enter_context(self.nc.semaphore(f"s{self._semid}_{tag}"))

    def _wait(self, E, deps, strict=False):
        for d in deps:
            if d is None:
                continue
            sem, val, src, kind = d
            if src == E.name and not strict:
                if kind in ("r", "ww") or E.name == "pe" or E.name == "sp":
                    continue
            key = id(sem)
            if E.waited.get(key, 0) >= val:
                continue
            E.h.wait_ge(sem, val)
            E.waited[key] = val

    def _deps(self, reads, writes):
        deps = []
        for b in reads:
            if b.w is not None:
                deps.append(b.w + ("w",))
        for b in writes:
            if b.w is not None:
                deps.append(b.w + ("ww",))
            for t in b.r.values():
                deps.append(t + ("r",))
        return deps

    def _commit(self, tok, reads, writes):
        for b in writes:
            b.w = tok
            b.r = {}
        for b in reads:
            if b in writes:
                continue
            key = id(tok[0])
            old = b.r.get(key)
            if old is None or old[1] < tok[1]:
                b.r[key] = tok

    def op(self, e, fn, reads=(), writes=()):
        E = self.eng[e]
        self._wait(E, self._deps(reads, writes))
        ins = fn(E.h)
        if E.sem is None or E.count >= SEM_LIMIT:
            E.sem = self._new_sem(E.name)
            E.count = 0
        ins.then_inc(E.sem, 1)
        E.count += 1
        tok = (E.sem, E.count, E.name)
        self._commit(tok, reads, writes)
        return tok

    def dma(self, q, fn, reads=(), writes=()):
        E = self.eng[q]
        self._wait(E, self._deps(reads, writes), strict=True)
        if q not in self.dma_sems:
            self.dma_sems[q] = []
            self.dma_rot[q] = 0
        pool = self.dma_sems[q]
        i = self.dma_rot[q]
        self.dma_rot[q] = (i + 1) % self.n_dma_sems
        if i >= len(pool):
            ent = [self._new_sem("d" + q), 0]
            pool.append(ent)
            self.all_dma.append(ent)
        ent = pool[i]
        if ent[1] >= SEM_LIMIT:
            self._wait(E, [(ent[0], ent[1], "dma", "w")])
            ent = [self._new_sem("d" + q), 0]
            pool[i] = ent
            self.all_dma.append(ent)
        if ent[1] > 0:
            self._wait(E, [(ent[0], ent[1], "dma", "w")])
        ins = fn(E.h)
        ins.then_inc(ent[0], 16)
        ent[1] += 16
        tok = (ent[0], ent[1], "dma")
        self._commit(tok, reads, writes)
        return tok

    def barrier(self):
        toks = []
        for E in self.eng.values():
            if E.sem is not None and E.count > 0:
                toks.append((E.sem, E.count, E.name, "w"))
        for ent in self.all_dma:
            if ent[1] > 0:
                toks.append((ent[0], ent[1], "dma", "w"))
        for E in self.eng.values():
            self._wait(E, [t for t in toks if t[2] != E.name])

    def finish(self, final_toks):
        E = self.eng["sp"]
        self._wait(E, [t + ("w",) for t in final_toks])


class Ctx:
    def __init__(self, nc, K, tag):
        self.nc = nc
        self.K = K
        self.tag = tag
        self.st = ExitStack()
        self.n = 0

    def sb(self, shape, dt, name="t"):
        self.n += 1
        return self.st.enter_context(self.nc.sbuf_tensor(f"{self.tag}_{name}{self.n}", list(shape), dt))

    def ps(self, shape, dt, name="p"):
        self.n += 1
        return self.st.enter_context(self.nc.psum_tensor(f"{self.tag}_{name}{self.n}", list(shape), dt))

    def close(self):
        self.K.barrier()
        self.st.close()


def bcast_mid(ap2d, n):
    return ap2d.unsqueeze(1).to_broadcast([ap2d.shape[0], n, ap2d.shape[1]])


def bcast_last(ap2d, n):
    return ap2d.unsqueeze(2).to_broadcast([ap2d.shape[0], ap2d.shape[1], n])


def layer_norm_tile(K, C, y, by, gain, bgain, bias, bbias, out, bout, scr):
    stt, bst, mv, bmv, rs, brs, nm, bnm = scr
    for q in range(4):
        K.op("dve", lambda e: e.bn_stats(out=stt[:, q, :], in_=y[:, q * 512:(q + 1) * 512]),
             reads=[by], writes=[bst])
    K.op("dve", lambda e: e.bn_aggr(out=mv[:], in_=stt[:].rearrange("p a b -> p (a b)")),
         reads=[bst], writes=[bmv])
    K.op("dve", lambda e: e.tensor_scalar(out=rs[:], in0=mv[:, 1:2], scalar1=LN_EPS, scalar2=None, op0=ALU.add),
         reads=[bmv], writes=[brs])
    K.op("act", lambda e: e.activation(out=rs[:], in_=rs[:], func=AF.Sqrt), reads=[brs], writes=[brs])
    K.op("dve", lambda e: e.reciprocal(out=rs[:], in_=rs[:]), reads=[brs], writes=[brs])
    K.op("dve", lambda e: e.scalar_tensor_tensor(out=nm[:], in0=mv[:, 0:1], scalar=-1.0, in1=rs[:],
                                                  op0=ALU.mult, op1=ALU.mult),
         reads=[bmv, brs], writes=[bnm])
    K.op("act", lambda e: e.activation(out=y[:], in_=y[:], func=AF.Identity, scale=rs[:], bias=nm[:]),
         reads=[by, brs, bnm], writes=[by])
    K.op("dve", lambda e: e.tensor_tensor(out=y[:], in0=y[:], in1=gain[:], op=ALU.mult),
         reads=[by, bgain], writes=[by])
    K.op("dve", lambda e: e.tensor_tensor(out=out[:], in0=y[:], in1=bias[:], op=ALU.add),
         reads=[by, bbias], writes=[bout])


def ln_scratch(C):
    stt = C.sb([P, 4, 6], F32, "stt")
    mv = C.sb([P, 2], F32, "mv")
    rs = C.sb([P, 1], F32, "rs")
    nm = C.sb([P, 1], F32, "nm")
    return (stt, Buf(), mv, Buf(), rs, Buf(), nm, Buf())


def load_rep(K, C, dram_row, name):
    t = C.sb([P, D_MODEL], F32, name)
    b = Buf()
    K.dma("sp", lambda e: e.dma_start(out=t[:], in_=dram_row.partition_broadcast(P)), writes=[b])
    return t, b


def peer_phase(nc, K, tag, NT, xin, xout, wq_d, keysT_d, tab, gain_d, bias_d, cst_d, NB=7, btab=None, GRP=1):
    C = Ctx(nc, K, tag)
    wq = C.sb([P, 16, D_MODEL], BF16, "wq"); bwq = Buf()
    keysT = C.sb([P, 16, P], BF16, "keysT"); bkeys = Buf()
    cst = C.sb([P, 144], F32, "cst"); bcst = Buf()
    ident = cst[:, 0:128]
    iota16 = cst[:, 128:144]
    K.dma("sp", lambda e: e.dma_start(out=cst[:], in_=cst_d), writes=[bcst])
    wq_v = wq_d.rearrange("(k p) n -> k p n", p=P)
    for k in range(16):
        K.dma("pool", lambda e: e.dma_start(out=wq[:, k, :], in_=wq_v[k]), writes=[bwq])
    K.dma("pool", lambda e: e.dma_start(out=keysT[:], in_=keysT_d), writes=[bkeys])
    gain, bgain = load_rep(K, C, gain_d, "gain")
    bias, bbias = load_rep(K, C, bias_d, "bias")

    xt = [C.sb([P, D_MODEL], F32, "xt") for _ in range(2)]; bxt = [Buf(), Buf()]
    xT = C.sb([P, 16, P], BF16, "xT"); bxT = Buf()
    qT = C.sb([P, 16, P], BF16, "qT"); bqT = Buf()
    sc = C.sb([P, 16, P], F32, "sc"); bsc = Buf()
    wk = [C.sb([P, P], F32, "wk") for _ in range(2)]; bwk = [Buf(), Buf()]
    sv = C.sb([P, 16, 16], F32, "sv"); bsv = Buf()
    si = C.sb([P, 16, 16], U32, "si"); bsi = Buf()
    sif = C.sb([P, 16, 16], F32, "sif"); bsif = Buf()
    cand = C.sb([P, 8, 256], F32, "cand"); bcand = Buf()
    cw = [C.sb([P, 256], F32, "cw") for _ in range(2)]; bcw = [Buf(), Buf()]
    best = C.sb([P, 8, 16], F32, "best"); bbest = Buf()
    pos = C.sb([P, 8, 16], U32, "pos"); bpos = Buf()
    ai = C.sb([P, 128], U32, "ai"); bai = Buf()
    bi = C.sb([P, 128], U32, "bi"); bbi = Buf()
    af = C.sb([P, 128], F32, "af"); baf = Buf()
    bf = C.sb([P, 128], F32, "bf"); bbf = Buf()
    oh = sc[:].rearrange("p a (b c) -> p (a b) c", c=16); boh = bsc
    sel0 = C.sb([P, 128], F32, "sel0"); bsel0 = Buf()
    sel1 = C.sb([P, 128], F32, "sel1"); bsel1 = Buf()
    idx = [C.sb([P, 128], I32, "idx") for _ in range(2)]; bidx = [Buf(), Buf()]
    gate = [C.sb([P, 8, 16], F32, "gate") for _ in range(2)]; bgate = [Buf(), Buf()]
    gs = C.sb([P, 8], F32, "gs"); bgs = Buf()
    hb = C.sb([P, 128], F32, "hb"); bhbs = [Buf() for _ in range(128)]
    gl = C.sb([P, 128], F32, "gl"); bgls = [Buf() for _ in range(128 // GRP)]
    gb = [C.sb([P, 2 * D_MODEL], BF16, "gb") for _ in range(NB)]; bgb = [Buf() for _ in range(NB)]
    dg = [C.sb([P, P], BF16, "dg") for _ in range(4)]; bdg = [Buf() for _ in range(4)]
    y = C.sb([P, D_MODEL], F32, "y"); by = Buf()
    lns = ln_scratch(C)

    psA = C.ps([P, 2, 512], F32, "psA"); bpsA = [Buf(), Buf()]
    psS = C.ps([P, 2, 512], F32, "psS"); bpsS = [Buf(), Buf()]
    psV = C.ps([P, 4, 512], F32, "psV"); bpsV = Buf()

    state = {"gi": 0, "bank": 0}
    tabdep = [btab] if btab is not None else []

    def route(n):
        x = xt[n % 2]; bx = bxt[n % 2]
        K.dma("sp", lambda e: e.dma_start(out=x[:], in_=xin[n * P:(n + 1) * P, :]), writes=[bx])
        for g4 in range(4):
            b = state["bank"]; state["bank"] ^= 1
            for j in range(4):
                k = g4 * 4 + j
                K.op("pe", lambda e: e.transpose(out=psA[:, b, j * P:(j + 1) * P], in_=x[:, k * P:(k + 1) * P],
                                                 identity=ident), reads=[bx, bcst], writes=[bpsA[b]])
            K.op("act", lambda e: e.activation(out=xT[:, g4 * 4:(g4 + 1) * 4, :].rearrange("p a b -> p (a b)"),
                                               in_=psA[:, b, :], func=AF.Copy), reads=[bpsA[b]], writes=[bxT])
            yield
        for g4 in range(4):
            b = state["bank"]; state["bank"] ^= 1
            for j in range(4):
                hp = g4 * 4 + j
                for k in range(16):
                    K.op("pe", lambda e: e.matmul(out=psA[:, b, j * P:(j + 1) * P], lhsT=wq[:, k, hp * P:(hp + 1) * P],
                                                  rhs=xT[:, k, :], start=(k == 0), stop=(k == 15)),
                         reads=[bwq, bxT], writes=[bpsA[b]])
                yield
            K.op("act", lambda e: e.activation(out=qT[:, g4 * 4:(g4 + 1) * 4, :].rearrange("p a b -> p (a b)"),
                                               in_=psA[:, b, :], func=AF.Copy), reads=[bpsA[b]], writes=[bqT])
        for g4 in range(4):
            b = g4 % 2
            for j in range(4):
                hp = g4 * 4 + j
                K.op("pe", lambda e: e.matmul(out=psS[:, b, j * P:(j + 1) * P], lhsT=qT[:, hp, :], rhs=keysT[:, hp, :],
                                              start=True, stop=True), reads=[bqT, bkeys], writes=[bpsS[b]])
            K.op("act", lambda e: e.activation(out=sc[:, g4 * 4:(g4 + 1) * 4, :].rearrange("p a b -> p (a b)"),
                                               in_=psS[:, b, :], func=AF.Copy), reads=[bpsS[b]], writes=[bsc])
            yield
        for hp in range(16):
            w = wk[hp % 2]; bw = bwk[hp % 2]
            K.op("dve", lambda e: e.max(out=sv[:, hp, 0:8], in_=sc[:, hp, :]), reads=[bsc], writes=[bsv])
            K.op("dve", lambda e: e.max_index(out=si[:, hp, 0:8], in_max=sv[:, hp, 0:8], in_values=sc[:, hp, :]),
                 reads=[bsc, bsv], writes=[bsi])
            K.op("dve", lambda e: e.match_replace(out=w[:], in_to_replace=sv[:, hp, 0:8], in_values=sc[:, hp, :],
                                                  imm_value=NEG), reads=[bsc, bsv], writes=[bw])
            K.op("dve", lambda e: e.max(out=sv[:, hp, 8:16], in_=w[:]), reads=[bw], writes=[bsv])
            K.op("dve", lambda e: e.max_index(out=si[:, hp, 8:16], in_max=sv[:, hp, 8:16], in_values=w[:]),
                 reads=[bw, bsv], writes=[bsi])
            yield
        K.op("dve", lambda e: e.tensor_copy(out=sif[:], in_=si[:]), reads=[bsi], writes=[bsif])
        sv4 = sv[:].rearrange("p (h t) k -> p h t k", t=2)
        for h in range(8):
            K.op("dve", lambda e: e.tensor_tensor(out=cand[:, h, :].rearrange("p (a b) -> p a b", b=16),
                                                  in0=bcast_last(sv4[:, h, 0, :], 16), in1=bcast_mid(sv4[:, h, 1, :], 16),
                                                  op=ALU.add), reads=[bsv], writes=[bcand])
        yield
        for h in range(8):
            w = cw[h % 2]; bw = bcw[h % 2]
            K.op("dve", lambda e: e.max(out=best[:, h, 0:8], in_=cand[:, h, :]), reads=[bcand], writes=[bbest])
            K.op("dve", lambda e: e.max_index(out=pos[:, h, 0:8], in_max=best[:, h, 0:8], in_values=cand[:, h, :]),
                 reads=[bcand, bbest], writes=[bpos])
            K.op("dve", lambda e: e.match_replace(out=w[:], in_to_replace=best[:, h, 0:8], in_values=cand[:, h, :],
                                                  imm_value=NEG), reads=[bcand, bbest], writes=[bw])
            K.op("dve", lambda e: e.max(out=best[:, h, 8:16], in_=w[:]), reads=[bw], writes=[bbest])
            K.op("dve", lambda e: e.max_index(out=pos[:, h, 8:16], in_max=best[:, h, 8:16], in_values=w[:]),
                 reads=[bw, bbest], writes=[bpos])
            yield
        posf = pos[:].rearrange("p h k -> p (h k)")
        K.op("dve", lambda e: e.tensor_scalar(out=ai[:], in0=posf, scalar1=4, scalar2=None, op0=ALU.logical_shift_right),
             reads=[bpos], writes=[bai])
        K.op("dve", lambda e: e.tensor_scalar(out=bi[:], in0=posf, scalar1=15, scalar2=None, op0=ALU.bitwise_and),
             reads=[bpos], writes=[bbi])
        K.op("dve", lambda e: e.tensor_copy(out=af[:], in_=ai[:]), reads=[bai], writes=[baf])
        K.op("dve", lambda e: e.tensor_copy(out=bf[:], in_=bi[:]), reads=[bbi], writes=[bbf])
        yield
        sif4 = sif[:].rearrange("p (h t) k -> p h t k", t=2)
        for (srcf, bsrc, t, sel, bsel) in ((af, baf, 0, sel0, bsel0), (bf, bbf, 1, sel1, bsel1)):
            K.op("dve", lambda e: e.tensor_tensor(out=oh[:], in0=bcast_mid(iota16, 128), in1=bcast_last(srcf[:], 16),
                                                  op=ALU.is_equal), reads=[bcst, bsrc], writes=[boh])
            for h in range(8):
                K.op("dve", lambda e: e.tensor_tensor(out=oh[:, h * 16:(h + 1) * 16, :], in0=oh[:, h * 16:(h + 1) * 16, :],
                                                      in1=bcast_mid(sif4[:, h, t, :], 16), op=ALU.mult),
                     reads=[boh, bsif], writes=[boh])
                if h % 2 == 1:
                    yield
            K.op("dve", lambda e: e.tensor_reduce(out=sel[:], in_=oh[:], axis=AX.X, op=ALU.add), reads=[boh], writes=[bsel])
            yield
        ix = idx[n % 2]; bix = bidx[n % 2]
        K.op("dve", lambda e: e.scalar_tensor_tensor(out=sel0[:], in0=sel0[:], scalar=128.0, in1=sel1[:],
                                                      op0=ALU.mult, op1=ALU.add), reads=[bsel0, bsel1], writes=[bsel0])
        K.op("dve", lambda e: e.tensor_copy(out=ix[:], in_=sel0[:]), reads=[bsel0], writes=[bix])
        yield
        g = gate[n % 2]; bg = bgate[n % 2]
        K.op("dve", lambda e: e.tensor_tensor(out=g[:], in0=best[:], in1=best[:, :, 0:1].to_broadcast([P, 8, 16]),
                                              op=ALU.subtract), reads=[bbest], writes=[bg])
        K.op("act", lambda e: e.activation(out=g[:], in_=g[:], func=AF.Exp), reads=[bg], writes=[bg])
        K.op("dve", lambda e: e.tensor_reduce(out=gs[:], in_=g[:], axis=AX.X, op=ALU.add), reads=[bg], writes=[bgs])
        K.op("dve", lambda e: e.reciprocal(out=gs[:], in_=gs[:]), reads=[bgs], writes=[bgs])
        K.op("dve", lambda e: e.tensor_tensor(out=g[:], in0=g[:], in1=bcast_last(gs[:], 16), op=ALU.mult),
             reads=[bg, bgs], writes=[bg])

    def experts(n, bg_gen=None):
        x = xt[n % 2]; bx = bxt[n % 2]
        ix = idx[n % 2]; bix = bidx[n % 2]
        g2 = gate[n % 2][:].rearrange("p h k -> p (h k)"); bg = bgate[n % 2]
        ngroups = 128 // GRP
        slot_buf = {}
        for G in range(ngroups + 1):
            if G < ngroups:
                for s in range(G * GRP, (G + 1) * GRP):
                    j = state["gi"] % NB; state["gi"] += 1
                    slot_buf[s] = j
                    K.dma("pool", lambda e: e.indirect_dma_start(out=gb[j][:, :], out_offset=None, in_=tab,
                                                                 in_offset=bass.IndirectOffsetOnAxis(ap=ix[:, s:s + 1], axis=0)),
                          reads=[bix] + tabdep, writes=[bgb[j]])
                    K.op("dve", lambda e: e.scalar_tensor_tensor(out=gb[j][:, 0:D_MODEL], in0=gb[j][:, 0:D_MODEL], scalar=1.0, in1=x[:],
                                                                  op0=ALU.mult, op1=ALU.mult, accum_out=hb[:, s:s + 1]),
                         reads=[bgb[j], bx], writes=[bhbs[s], bgb[j]])
                K.op("act", lambda e: e.activation(out=gl[:, G * GRP:(G + 1) * GRP], in_=hb[:, G * GRP:(G + 1) * GRP], func=AF.Gelu),
                     reads=[bhbs[s] for s in range(G * GRP, (G + 1) * GRP)], writes=[bgls[G]])
            if G >= 1:
                for s in range((G - 1) * GRP, G * GRP):
                    j = slot_buf.pop(s)
                    d = dg[s % 4]; bd = bdg[s % 4]
                    K.op("dve", lambda e: e.tensor_scalar(out=d[:], in0=ident, scalar1=gl[:, s:s + 1], scalar2=g2[:, s:s + 1],
                                                          op0=ALU.mult, op1=ALU.mult),
                         reads=[bcst, bgls[G - 1], bg], writes=[bd])
                    for q in range(4):
                        K.op("pe", lambda e: e.matmul(out=psV[:, q, :], lhsT=d[:],
                                                      rhs=gb[j][:, D_MODEL + q * 512:D_MODEL + (q + 1) * 512],
                                                      start=(s == 0), stop=(s == 127)), reads=[bd, bgb[j]], writes=[bpsV])
            if bg_gen is not None:
                next(bg_gen, None)
        K.op("dve", lambda e: e.scalar_tensor_tensor(out=y[:], in0=x[:], scalar=ALPHA, in1=psV[:].rearrange("p a b -> p (a b)"),
                                                      op0=ALU.mult, op1=ALU.add), reads=[bx, bpsV], writes=[by])
        layer_norm_tile(K, C, y, by, gain, bgain, bias, bbias, y, by, lns)
        return K.dma("sp", lambda e: e.dma_start(out=xout[n * P:(n + 1) * P, :], in_=y[:]), reads=[by])

    toks = []
    for _ in route(0):
        pass
    for n in range(NT):
        gen = route(n + 1) if n + 1 < NT else None
        toks.append(experts(n, gen))
        if gen is not None:
            for _ in gen:
                pass
    C.close()
    return toks


def const_tables():
    cst = np.zeros((P, 144), np.float32)
    cst[:, :128] = np.eye(128, dtype=np.float32)
    cst[:, 128:144] = np.arange(16, dtype=np.float32)[None, :]
    slopes = np.array([2.0 ** (-8.0 * (h + 1) / N_HEADS) for h in range(N_HEADS)], dtype=np.float32)
    s = np.arange(P)[:, None]
    q = np.arange(P)[None, :]
    ab = np.zeros((P, 2, N_HEADS, P), np.float32)
    dist0 = (q + 128 - s).astype(np.float32)
    dist1 = (q - s).astype(np.float32)
    for h in range(N_HEADS):
        ab[:, 0, h, :] = np.where(s > q, -slopes[h] * dist0, -30000.0)
        ab[:, 1, h, :] = np.where(q >= s, -slopes[h] * dist1, -30000.0)
    pm = np.zeros((12, P, P), np.float32)
    for g, w in enumerate(POOL_WINDOWS):
        for t in range(P):
            for sidx in range(max(0, t - w + 1), t + 1):
                pm[g, sidx, t] += 1.0 / w
            pm[g, t, t] -= 1.0
            for sp_ in range(P):
                if sp_ - 128 >= t - w + 1:
                    pm[4 + g, sp_, t] += 1.0 / w
            cnt = min(t + 1, w)
            for sidx in range(max(0, t - w + 1), t + 1):
                pm[8 + g, sidx, t] += np.float32(1.0) / np.float32(cnt)
            pm[8 + g, t, t] -= 1.0
    return cst, ab, pm


def pool_phase(nc, K, tag, NT, xin, xout, poolw_d, scale_d, gain_d, bias_d, pm_d):
    C = Ctx(nc, K, tag)
    wp = C.sb([P, 4, 4, 512], BF16, "wp"); bwp = Buf()
    for g in range(4):
        K.dma("pool", lambda e: e.dma_start(out=wp[:, g, :, :], in_=poolw_d[g].rearrange("(cc p) n -> p cc n", p=P)),
              writes=[bwp])
    pm = C.sb([P, 12, P], F32, "pm"); bpm = Buf()
    K.dma("sp", lambda e: e.dma_start(out=pm[:], in_=pm_d.rearrange("m s t -> s m t")), writes=[bpm])
    gain, bgain = load_rep(K, C, gain_d, "gain")
    bias, bbias = load_rep(K, C, bias_d, "bias")
    scl, bscl = load_rep(K, C, scale_d, "scl")
    xt = [C.sb([P, D_MODEL], F32, "xt") for _ in range(2)]; bxt = [Buf(), Buf()]
    mixT = C.sb([P, 16, P], BF16, "mixT"); bmixT = Buf()
    y = C.sb([P, D_MODEL], F32, "y"); by = Buf()
    ot = C.sb([P, D_MODEL], F32, "ot"); bot = Buf()
    lns = ln_scratch(C)
    psM = C.ps([P, 4, 512], F32, "psM"); bpsM = [Buf() for _ in range(4)]
    psY = C.ps([P, 4, 512], F32, "psY"); bpsY = Buf()
    toks = []
    for n in range(NT):
        x = xt[n % 2]; bx = bxt[n % 2]
        xp = xt[(n - 1) % 2]; bxp = bxt[(n - 1) % 2]
        K.dma("sp", lambda e: e.dma_start(out=x[:], in_=xin[n * P:(n + 1) * P, :]), writes=[bx])
        for g in range(4):
            mc = pm[:, (8 + g) if n == 0 else g, :]
            mp = pm[:, 4 + g, :]
            for cc in range(4):
                k = g * 4 + cc
                K.op("pe", lambda e: e.matmul(out=psM[:, g, cc * P:(cc + 1) * P], lhsT=x[:, k * P:(k + 1) * P], rhs=mc,
                                              start=True, stop=(n == 0)), reads=[bx, bpm], writes=[bpsM[g]])
                if n > 0:
                    K.op("pe", lambda e: e.matmul(out=psM[:, g, cc * P:(cc + 1) * P], lhsT=xp[:, k * P:(k + 1) * P], rhs=mp,
                                                  start=False, stop=True), reads=[bxp, bpm], writes=[bpsM[g]])
            K.op("act", lambda e: e.activation(out=mixT[:, g * 4:(g + 1) * 4, :].rearrange("p a b -> p (a b)"),
                                               in_=psM[:, g, :], func=AF.Copy), reads=[bpsM[g]], writes=[bmixT])
        for g in range(4):
            for cc in range(4):
                K.op("pe", lambda e: e.matmul(out=psY[:, g, :], lhsT=mixT[:, g * 4 + cc, :], rhs=wp[:, g, cc, :],
                                              start=(cc == 0), stop=(cc == 3)), reads=[bmixT, bwp], writes=[bpsY])
        K.op("dve", lambda e: e.tensor_tensor(out=y[:], in0=psY[:].rearrange("p a b -> p (a b)"), in1=scl[:], op=ALU.mult),
             reads=[bpsY, bscl], writes=[by])
        K.op("dve", lambda e: e.scalar_tensor_tensor(out=y[:], in0=x[:], scalar=ALPHA, in1=y[:], op0=ALU.mult, op1=ALU.add),
             reads=[bx, by], writes=[by])
        layer_norm_tile(K, C, y, by, gain, bgain, bias, bbias, ot, bot, lns)
        toks.append(K.dma("sp", lambda e: e.dma_start(out=xout[n * P:(n + 1) * P, :], in_=ot[:]), reads=[bot]))
    C.close()
    return toks


def attn_phase1(nc, K, tag, NT, xin, attnT_d, wqkv_d, sinks_d, cst_d, abias_d):
    C = Ctx(nc, K, tag)
    wqkv = C.sb([P, 16, QKV_DIM], BF16, "wqkv"); bw = Buf()
    wv = wqkv_d.rearrange("(k p) n -> k p n", p=P)
    for k in range(16):
        for hh in range(2):
            K.dma("pool", lambda e: e.dma_start(out=wqkv[:, k, hh * 1280:(hh + 1) * 1280], in_=wv[k][:, hh * 1280:(hh + 1) * 1280]),
                  writes=[bw])
    cst = C.sb([P, 144], F32, "cst"); bcst = Buf()
    K.dma("sp", lambda e: e.dma_start(out=cst[:], in_=cst_d), writes=[bcst])
    ident = cst[:, 0:128]
    identb = C.sb([P, P], BF16, "identb"); bidb = Buf()
    K.op("dve", lambda e: e.tensor_copy(out=identb[:], in_=ident), reads=[bcst], writes=[bidb])
    abias = C.sb([P, 2, N_HEADS, P], F32, "abias"); bab = Buf()
    K.dma("sp", lambda e: e.dma_start(out=abias[:], in_=abias_d), writes=[bab])
    esink = C.sb([P, N_HEADS], F32, "esink"); bes = Buf()
    K.dma("sp", lambda e: e.dma_start(out=esink[:], in_=sinks_d.partition_broadcast(P)), writes=[bes])
    K.op("act", lambda e: e.activation(out=esink[:], in_=esink[:], func=AF.Exp), reads=[bes], writes=[bes])

    xt = [C.sb([P, D_MODEL], F32, "xt") for _ in range(2)]; bxt = [Buf(), Buf()]
    xT = C.sb([P, 16, P], BF16, "xT"); bxT = Buf()
    QT = C.sb([64, N_HEADS, P], BF16, "QT"); bQT = Buf()
    KT = [C.sb([64, N_KV, P], BF16, "KT") for _ in range(2)]; bKT = [Buf(), Buf()]
    Va = [C.sb([P, N_KV, 65], BF16, "Va") for _ in range(2)]; bVa = [Buf(), Buf()]
    for i in range(2):
        K.op("dve", lambda e: e.memset(Va[i][:], 1.0), writes=[bVa[i]])
    tmp = [C.sb([P, 1024], F32, "tmp") for _ in range(2)]; btmp = [Buf(), Buf()]
    E = [C.sb([P, 8, P], BF16, "E") for _ in range(2)]; bE = [Buf(), Buf()]
    rec = C.sb([P, 2, 4], F32, "rec"); brec = Buf()
    attn = C.sb([P, N_HEADS, 64], BF16, "attn"); battn = Buf()
    aT = [C.sb([P, 16 * P], BF16, "aT") for _ in range(2)]; baT = [Buf(), Buf()]

    psA = C.ps([P, 2, 512], F32, "psA"); bpsA = [Buf(), Buf()]
    psS = C.ps([P, 2, 1024], F32, "psS"); bpsS = [Buf(), Buf()]
    psO = C.ps([P, 2, 512], F32, "psO"); bpsO = Buf()
    bank = [0]

    def nb():
        b = bank[0]; bank[0] ^= 1
        return b

    for n in range(NT):
        x = xt[n % 2]; bx = bxt[n % 2]
        cur = n % 2; prv = (n - 1) % 2
        K.dma("sp", lambda e: e.dma_start(out=x[:], in_=xin[n * P:(n + 1) * P, :]), writes=[bx])
        for g4 in range(4):
            b = nb()
            for j in range(4):
                k = g4 * 4 + j
                K.op("pe", lambda e: e.transpose(out=psA[:, b, j * P:(j + 1) * P], in_=x[:, k * P:(k + 1) * P], identity=ident),
                     reads=[bx, bcst], writes=[bpsA[b]])
            K.op("act", lambda e: e.activation(out=xT[:, g4 * 4:(g4 + 1) * 4, :].rearrange("p a b -> p (a b)"),
                                               in_=psA[:, b, :], func=AF.Copy), reads=[bpsA[b]], writes=[bxT])
        for g4 in range(8):
            b = nb()
            for j in range(4):
                h = g4 * 4 + j
                for k in range(16):
                    K.op("pe", lambda e: e.matmul(out=psA[0:64, b, j * P:(j + 1) * P], lhsT=wqkv[:, k, h * 64:(h + 1) * 64],
                                                  rhs=xT[:, k, :], start=(k == 0), stop=(k == 15)),
                         reads=[bw, bxT], writes=[bpsA[b]])
            K.op("act", lambda e: e.activation(out=QT[:, g4 * 4:(g4 + 1) * 4, :].rearrange("p a b -> p (a b)"),
                                               in_=psA[0:64, b, :], func=AF.Copy), reads=[bpsA[b]], writes=[bQT])
        b = nb()
        for g in range(N_KV):
            for k in range(16):
                K.op("pe", lambda e: e.matmul(out=psA[0:64, b, g * P:(g + 1) * P], lhsT=wqkv[:, k, 2048 + g * 64:2048 + (g + 1) * 64],
                                              rhs=xT[:, k, :], start=(k == 0), stop=(k == 15)),
                     reads=[bw, bxT], writes=[bpsA[b]])
        K.op("act", lambda e: e.activation(out=KT[cur][:].rearrange("p a b -> p (a b)"), in_=psA[0:64, b, :], func=AF.Copy),
             reads=[bpsA[b]], writes=[bKT[cur]])
        b = nb()
        for k in range(16):
            K.op("pe", lambda e: e.matmul(out=psA[:, b, 0:256], lhsT=xT[:, k, :], rhs=wqkv[:, k, 2304:2560],
                                          start=(k == 0), stop=(k == 15)), reads=[bw, bxT], writes=[bpsA[b]])
        K.op("act", lambda e: e.activation(out=Va[cur][:, :, 0:64], in_=psA[:, b, 0:256].rearrange("p (g d) -> p g d", d=64),
                                           func=AF.Copy), reads=[bpsA[b]], writes=[bVa[cur]])
        blks = ([(0, prv)] if n > 0 else []) + [(1, cur)]
        for g in range(N_KV):
            for (blk, buf) in blks:
                for hl in range(8):
                    h = g * 8 + hl
                    K.op("pe", lambda e: e.matmul(out=psS[:, blk, hl * P:(hl + 1) * P], lhsT=KT[buf][:, g, :], rhs=QT[:, h, :],
                                                  start=True, stop=True), reads=[bKT[buf], bQT], writes=[bpsS[blk]])
                K.op("dve", lambda e: e.scalar_tensor_tensor(out=tmp[blk][:], in0=psS[:, blk, :], scalar=0.125,
                                                              in1=abias[:, blk, g * 8:(g + 1) * 8, :].rearrange("p a b -> p (a b)"),
                                                              op0=ALU.mult, op1=ALU.add),
                     reads=[bpsS[blk], bab], writes=[btmp[blk]])
                K.op("act", lambda e: e.activation(out=E[blk][:].rearrange("p a b -> p (a b)"), in_=tmp[blk][:], func=AF.Exp),
                     reads=[btmp[blk]], writes=[bE[blk]])
            for hl in range(8):
                o = psO[:, hl // 4, (hl % 4) * 65:(hl % 4) * 65 + 65]
                for i, (blk, buf) in enumerate(blks):
                    K.op("pe", lambda e: e.matmul(out=o, lhsT=E[blk][:, hl, :], rhs=Va[buf][:, g, :],
                                                  start=(i == 0), stop=(i == len(blks) - 1)),
                         reads=[bE[blk], bVa[buf]], writes=[bpsO])
            ov = psO[:, :, 0:260].rearrange("p b (h e) -> p b h e", e=65)
            K.op("dve", lambda e: e.tensor_tensor(out=rec[:], in0=ov[:, :, :, 64],
                                                  in1=esink[:, g * 8:(g + 1) * 8].rearrange("p (b h) -> p b h", h=4), op=ALU.add),
                 reads=[bpsO, bes], writes=[brec])
            K.op("dve", lambda e: e.reciprocal(out=rec[:], in_=rec[:]), reads=[brec], writes=[brec])
            for bk in range(2):
                K.op("dve", lambda e: e.tensor_tensor(out=attn[:, g * 8 + bk * 4:g * 8 + bk * 4 + 4, :], in0=ov[:, bk, :, 0:64],
                                                      in1=bcast_last(rec[:, bk, :], 64), op=ALU.mult),
                     reads=[bpsO, brec], writes=[battn])
        at = aT[n % 2]; bat = baT[n % 2]
        af2 = attn[:].rearrange("p h d -> p (h d)")
        for g8 in range(2):
            b = nb()
            pb = psA[:, b, :].bitcast(BF16)
            for j in range(8):
                k = g8 * 8 + j
                K.op("pe", lambda e: e.transpose(out=pb[:, j * P:(j + 1) * P], in_=af2[:, k * P:(k + 1) * P], identity=identb[:]),
                     reads=[battn, bidb], writes=[bpsA[b]])
            K.op("act", lambda e: e.activation(out=at[:, g8 * 1024:(g8 + 1) * 1024], in_=pb, func=AF.Copy),
                 reads=[bpsA[b]], writes=[bat])
        K.dma("sp", lambda e: e.dma_start(out=attnT_d[n], in_=at[:]), reads=[bat])
    C.close()


def attn_phase2(nc, K, tag, NT, xin, xout, attnT_d, wo_d, gain_d, bias_d):
    C = Ctx(nc, K, tag)
    wo = C.sb([P, 16, D_MODEL], BF16, "wo"); bwo = Buf()
    wv = wo_d.rearrange("(k p) n -> k p n", p=P)
    for k in range(16):
        K.dma("pool", lambda e: e.dma_start(out=wo[:, k, :], in_=wv[k]), writes=[bwo])
    gain, bgain = load_rep(K, C, gain_d, "gain")
    bias, bbias = load_rep(K, C, bias_d, "bias")
    xt = [C.sb([P, D_MODEL], F32, "xt") for _ in range(2)]; bxt = [Buf(), Buf()]
    aT = [C.sb([P, 16, P], BF16, "aT") for _ in range(2)]; baT = [Buf(), Buf()]
    y = C.sb([P, D_MODEL], F32, "y"); by = Buf()
    ot = C.sb([P, D_MODEL], F32, "ot"); bot = Buf()
    lns = ln_scratch(C)
    psY = [C.ps([P, 4, 512], F32, "psY") for _ in range(2)]; bpsY = [Buf(), Buf()]
    toks = []
    for n in range(NT):
        x = xt[n % 2]; bx = bxt[n % 2]
        a = aT[n % 2]; ba = baT[n % 2]
        py = psY[n % 2]; bpy = bpsY[n % 2]
        K.dma("sp", lambda e: e.dma_start(out=x[:], in_=xin[n * P:(n + 1) * P, :]), writes=[bx])
        K.dma("sp", lambda e: e.dma_start(out=a[:].rearrange("p a b -> p (a b)"), in_=attnT_d[n]), writes=[ba])
        for q in range(4):
            for k in range(16):
                K.op("pe", lambda e: e.matmul(out=py[:, q, :], lhsT=a[:, k, :], rhs=wo[:, k, q * 512:(q + 1) * 512],
                                              start=(k == 0), stop=(k == 15)), reads=[ba, bwo], writes=[bpy])
        K.op("dve", lambda e: e.scalar_tensor_tensor(out=y[:], in0=x[:], scalar=ALPHA, in1=py[:].rearrange("p a b -> p (a b)"),
                                                      op0=ALU.mult, op1=ALU.add), reads=[bx, bpy], writes=[by])
        layer_norm_tile(K, C, y, by, gain, bgain, bias, bbias, ot, bot, lns)
        toks.append(K.dma("sp", lambda e: e.dma_start(out=xout[n * P:(n + 1) * P, :], in_=ot[:]), reads=[bot]))
    C.close()
    return toks


def build_program(NT=16):
    nc = bass.Bass("TRN2", target_bir_lowering=False)
    T = NT * P
    x_d = nc.dram_tensor("x", [T, D_MODEL], F32, kind="ExternalInput")
    wqkv_d = nc.dram_tensor("wqkv", [D_MODEL, QKV_DIM], F32, kind="ExternalInput")
    wo_d = nc.dram_tensor("wo", [D_MODEL, D_MODEL], F32, kind="ExternalInput")
    sinks_d = nc.dram_tensor("sinks", [N_HEADS], F32, kind="ExternalInput")
    poolw_d = nc.dram_tensor("poolw", [4, 512, 512], F32, kind="ExternalInput")
    pscale_d = nc.dram_tensor("pscale", [D_MODEL], F32, kind="ExternalInput")
    gain_d = nc.dram_tensor("lng", [4, D_MODEL], F32, kind="ExternalInput")
    bias_d = nc.dram_tensor("lnb", [4, D_MODEL], F32, kind="ExternalInput")
    wq_d = [nc.dram_tensor(f"wq{l}", [D_MODEL, D_MODEL], F32, kind="ExternalInput") for l in range(2)]
    keys_d = [nc.dram_tensor(f"keysT{l}", [P, 16, P], F32, kind="ExternalInput") for l in range(2)]
    u_d = [nc.dram_tensor(f"u{l}", [N_EXPERTS, D_MODEL], F32, kind="ExternalInput") for l in range(2)]
    v_d = [nc.dram_tensor(f"v{l}", [N_EXPERTS, D_MODEL], F32, kind="ExternalInput") for l in range(2)]
    cst_d = nc.dram_tensor("cst", [P, 144], F32, kind="ExternalInput")
    ab_d = nc.dram_tensor("abias", [P, 2, N_HEADS, P], F32, kind="ExternalInput")
    pm_d = nc.dram_tensor("poolm", [12, P, P], F32, kind="ExternalInput")
    out_d = nc.dram_tensor("out", [T, D_MODEL], F32, kind="ExternalOutput")
    attnT_d = nc.dram_tensor("attnT_scr", [NT, P, 16 * P], BF16, kind="Internal")
    xa_d = nc.dram_tensor("xa_scr", [T, D_MODEL], F32, kind="Internal")
    xb_d = nc.dram_tensor("xb_scr", [T, D_MODEL], F32, kind="Internal")
    xc_d = nc.dram_tensor("xc_scr", [T, D_MODEL], F32, kind="Internal")
    w16 = [nc.dram_tensor(f"w16_{l}", [N_EXPERTS, 2 * D_MODEL], BF16, kind="Internal") for l in range(2)]
    with ExitStack() as st:
        K = Kern(nc, st)
        btab = [Buf(), Buf()]
        for l in range(2):
            for (src, c0) in ((u_d[l], 0), (v_d[l], D_MODEL)):
                for r in range(0, N_EXPERTS, 1024):
                    K.dma("pool", lambda e: e.dma_start(out=w16[l][r:r + 1024, c0:c0 + D_MODEL], in_=src[r:r + 1024, :]),
                          writes=[btab[l]])
        attn_phase1(nc, K, "a1", NT, x_d.ap(), attnT_d.ap(), wqkv_d.ap(), sinks_d.ap(), cst_d.ap(), ab_d.ap())
        attn_phase2(nc, K, "a2", NT, x_d.ap(), xa_d.ap(), attnT_d.ap(), wo_d.ap(), gain_d[0, :], bias_d[0, :])
        peer_phase(nc, K, "p0", NT, xa_d.ap(), xb_d.ap(), wq_d[0].ap(), keys_d[0].ap(), w16[0].ap(),
                   gain_d[1, :], bias_d[1, :], cst_d.ap(), btab=btab[0])
        pool_phase(nc, K, "c1", NT, xb_d.ap(), xc_d.ap(), poolw_d.ap(), pscale_d.ap(), gain_d[2, :], bias_d[2, :], pm_d.ap())
        toks = peer_phase(nc, K, "p1", NT, xc_d.ap(), out_d.ap(), wq_d[1].ap(), keys_d[1].ap(), w16[1].ap(),
                          gain_d[3, :], bias_d[3, :], cst_d.ap(), btab=btab[1])
        K.finish(toks)
    return nc


def kernel(x, attn_w_qkv, attn_w_o, attn_sinks, pool_w, pool_scale, ln_gain, ln_bias,
           peer_w_query, peer_sub_keys, peer_u, peer_v):
    f32 = np.float32
    x = np.asarray(x, f32)
    cst, ab, pm = const_tables()
    shared = {
        "wqkv": np.ascontiguousarray(np.asarray(attn_w_qkv, f32)[0]),
        "wo": np.ascontiguousarray(np.asarray(attn_w_o, f32)[0]),
        "sinks": np.ascontiguousarray(np.asarray(attn_sinks, f32)[0]),
        "poolw": np.ascontiguousarray(np.asarray(pool_w, f32)[0]),
        "pscale": np.ascontiguousarray(np.asarray(pool_scale, f32)[0]),
        "lng": np.ascontiguousarray(np.asarray(ln_gain, f32).reshape(4, D_MODEL)),
        "lnb": np.ascontiguousarray(np.asarray(ln_bias, f32).reshape(4, D_MODEL)),
        "cst": cst, "abias": ab, "poolm": pm,
    }
    sk = np.asarray(peer_sub_keys, f32)
    for l in range(2):
        shared[f"wq{l}"] = np.ascontiguousarray(np.asarray(peer_w_query, f32)[l])
        shared[f"keysT{l}"] = np.ascontiguousarray(sk[l].transpose(3, 0, 1, 2).reshape(P, 16, P))
        shared[f"u{l}"] = np.ascontiguousarray(np.asarray(peer_u, f32)[l])
        shared[f"v{l}"] = np.ascontiguousarray(np.asarray(peer_v, f32)[l])
    nc = build_program(16)
    in_maps = []
    for b in range(BATCH):
        m = dict(shared)
        m["x"] = np.ascontiguousarray(x[b])
        in_maps.append(m)
    res = run_bass_kernel_spmd(nc, in_maps, core_ids=list(range(BATCH)))
    return np.stack([np.asarray(r["out"], f32) for r in res.results], axis=0)
```
